# Optimizing a Trainium2 kernel written in Bass

```python
import jax, jax.numpy as jnp
from jax import lax
import numpy as np


D_MODEL = 1024
BATCH = 16
SEQ = 2048
DEPTH = 2

DN_HEADS = 4
DN_HEAD_DIM = 128
DN_WIDTH = DN_HEADS * DN_HEAD_DIM
DN_CONV = 4
DN_CHUNK = 64
SB_HEADS = 4
SB_HEAD_DIM = 64
SB_WIDTH = SB_HEADS * SB_HEAD_DIM
SB_BLOCK = 128
SG_GROUPS = 4
SG_GROUP_DIM = 64
SG_WIDTH = SG_GROUPS * SG_GROUP_DIM
SG_CHUNK = 128
MIX_WIDTH = DN_WIDTH + SB_WIDTH + SG_WIDTH
D_FF = 4 * D_MODEL
IN_SPLITS = (3 * DN_WIDTH, DN_WIDTH, DN_HEADS, DN_HEADS, 3 * SB_WIDTH, 2 * SG_WIDTH)
IN_DIM = int(sum(IN_SPLITS))
IN_SPLIT_IDX = tuple(int(i) for i in np.cumsum(IN_SPLITS)[:-1])
NORM_EPS = 1e-6

kernel_name = 'hybrid_deltanet_stickbreaking_sgmlp_block'


def rms_norm(x, gain):
    xf = x.astype(jnp.float32)
    y = xf * lax.rsqrt(jnp.mean(xf * xf, axis=-1, keepdims=True) + NORM_EPS)
    return (y * gain.astype(jnp.float32)).astype(x.dtype)


def l2_norm(x):
    xf = x.astype(jnp.float32)
    return xf * lax.rsqrt(jnp.sum(xf * xf, axis=-1, keepdims=True) + NORM_EPS)


def causal_depthwise_conv(x, w):
    K, C = w.shape
    return lax.conv_general_dilated(
        x, w[:, None, :].astype(x.dtype), window_strides=(1,), padding=[(K - 1, 0)],
        dimension_numbers=('NWC', 'WIO', 'NWC'), feature_group_count=C)


def gated_delta_rule(q, k, v, g, beta):
    f32 = jnp.float32
    q, k, v, g, beta = (t.astype(f32) for t in (q, k, v, g, beta))
    B, T, H, Dk = q.shape
    Dv = v.shape[-1]
    C = DN_CHUNK
    N = T // C
    q = q * (Dk ** -0.5)

    def to_chunks(t):
        return jnp.moveaxis(t.reshape((B, N, C) + t.shape[2:]), 3, 2)

    qc, kc, vc, gc, bc = (to_chunks(t) for t in (q, k, v, g, beta))
    gcum = jnp.cumsum(gc, axis=-1)
    tril_incl = jnp.tril(jnp.ones((C, C), bool))
    tril_strict = jnp.tril(jnp.ones((C, C), bool), -1)
    decay = jnp.exp(jnp.where(tril_incl, gcum[..., :, None] - gcum[..., None, :], -jnp.inf))
    kk = jnp.einsum('bnhid,bnhjd->bnhij', kc, kc)
    lower = jnp.where(tril_strict, bc[..., :, None] * kk * decay, 0.0)
    a_mat = lower + jnp.eye(C, dtype=f32)
    rhs = jnp.concatenate([vc * bc[..., None], kc * (bc * jnp.exp(gcum))[..., None]], axis=-1)
    sol = lax.linalg.triangular_solve(a_mat, rhs, left_side=True, lower=True, unit_diagonal=True)
    u_val, w_dec = sol[..., :Dv], sol[..., Dv:]
    qk = jnp.einsum('bnhid,bnhjd->bnhij', qc, kc) * decay
    q_dec = qc * jnp.exp(gcum)[..., None]
    k_dec = kc * jnp.exp(gcum[..., -1:] - gcum)[..., None]
    chunk_decay = jnp.exp(gcum[..., -1])

    def step(S, inp):
        u_n, w_n, qk_n, qd_n, kd_n, cd_n = inp
        u_new = u_n - jnp.einsum('bhcd,bhde->bhce', w_n, S)
        o = jnp.einsum('bhcd,bhde->bhce', qd_n, S) + jnp.einsum('bhij,bhje->bhie', qk_n, u_new)
        S = S * cd_n[..., None, None] + jnp.einsum('bhcd,bhce->bhde', kd_n, u_new)
        return S, o

    xs = tuple(jnp.moveaxis(t, 1, 0) for t in (u_val, w_dec, qk, q_dec, k_dec, chunk_decay))
    S0 = jnp.zeros((B, H, Dk, Dv), f32)
    _, o = lax.scan(step, S0, xs)
    o = jnp.moveaxis(jnp.moveaxis(o, 0, 1), 2, 3).reshape(B, T, H, Dv)
    return o


def stick_breaking_attention(q, k, v):
    B, T, H, D = q.shape
    scale = D ** -0.5
    outs = []
    for blk in range(T // SB_BLOCK):
        q0 = blk * SB_BLOCK
        q1 = q0 + SB_BLOCK
        qb, kb, vb = q[:, q0:q1], k[:, :q1], v[:, :q1]
        z = jnp.einsum('bthd,bshd->bhts', qb, kb).astype(jnp.float32) * scale
        t_idx = q0 + jnp.arange(SB_BLOCK)
        s_idx = jnp.arange(q1)
        mask = s_idx[None, :] < t_idx[:, None]
        log_1m = jnp.where(mask, jax.nn.log_sigmoid(-z), 0.0)
        remaining = lax.cumsum(log_1m, axis=3, reverse=True) - log_1m
        weights = jnp.where(mask, jnp.exp(jax.nn.log_sigmoid(z) + remaining), 0.0)
        outs.append(jnp.einsum('bhts,bshd->bthd', weights.astype(v.dtype), vb))
    return jnp.concatenate(outs, axis=1)


def chunked_spatial_gating(u, v, v_gain, w_s, b_s):
    B, T, _ = u.shape
    N = T // SG_CHUNK
    u = jax.nn.gelu(u)
    v = jax.nn.gelu(v).reshape(B, T, SG_GROUPS, SG_GROUP_DIM)
    v = rms_norm(v, v_gain.reshape(SG_GROUPS, SG_GROUP_DIM))
    v = v.reshape(B, N, SG_CHUNK, SG_GROUPS, SG_GROUP_DIM)
    w = jnp.where(jnp.tril(jnp.ones((SG_CHUNK, SG_CHUNK), bool)), w_s, 0.0).astype(v.dtype)
    mixed = jnp.einsum('gts,bnsgd->bntgd', w, v) + b_s.T[None, None, :, :, None]
    return u * mixed.reshape(B, T, SG_WIDTH)


def hybrid_mixer(h, w_in, conv_w, a_log, dt_bias, dn_out_g, sb_q_g, sb_k_g, sg_v_g, sg_w, sg_b, w_out):
    B, T, _ = h.shape
    proj = h @ w_in
    dn_qkv, dn_z, dn_a, dn_b, sb_qkv, sg_uv = jnp.split(proj, IN_SPLIT_IDX, axis=-1)
    dn_qkv = jax.nn.silu(causal_depthwise_conv(dn_qkv, conv_w))
    q, k, v = (t.reshape(B, T, DN_HEADS, DN_HEAD_DIM) for t in jnp.split(dn_qkv, 3, axis=-1))
    g = -jnp.exp(a_log.astype(jnp.float32)) * jax.nn.softplus(dn_a.astype(jnp.float32) + dt_bias.astype(jnp.float32))
    beta = jax.nn.sigmoid(dn_b.astype(jnp.float32))
    o_dn = gated_delta_rule(l2_norm(q), l2_norm(k), v, g, beta).astype(h.dtype)
    o_dn = rms_norm(o_dn, dn_out_g) * jax.nn.silu(dn_z.reshape(B, T, DN_HEADS, DN_HEAD_DIM))
    o_dn = o_dn.reshape(B, T, DN_WIDTH)
    sq, sk, sv = (t.reshape(B, T, SB_HEADS, SB_HEAD_DIM) for t in jnp.split(sb_qkv, 3, axis=-1))
    o_sb = stick_breaking_attention(rms_norm(sq, sb_q_g), rms_norm(sk, sb_k_g), sv).reshape(B, T, SB_WIDTH)
    su, svv = jnp.split(sg_uv, 2, axis=-1)
    o_sg = chunked_spatial_gating(su, svv, sg_v_g, sg_w, sg_b)
    return jnp.concatenate([o_dn, o_sb, o_sg], axis=-1) @ w_out


def setup_inputs(seed: int = 0) -> dict:
    key = jax.random.key(seed)
    ks = jax.random.split(key, 20)
    f32 = jnp.float32
    nrm = lambda k, shape, s: jax.random.normal(k, shape, f32) * s
    dt = jnp.exp(jax.random.uniform(ks[5], (DEPTH, DN_HEADS), f32, np.log(1e-3), np.log(1e-1)))
    return {
        'x': nrm(ks[0], (BATCH, SEQ, D_MODEL), 1.0),
        'norm1_g': 1.0 + nrm(ks[1], (DEPTH, D_MODEL), 0.1),
        'w_in': nrm(ks[2], (DEPTH, D_MODEL, IN_DIM), D_MODEL ** -0.5),
        'conv_w': nrm(ks[3], (DEPTH, DN_CONV, 3 * DN_WIDTH), DN_CONV ** -0.5),
        'a_log': jnp.log(jax.random.uniform(ks[4], (DEPTH, DN_HEADS), f32, 1.0, 16.0)),
        'dt_bias': jnp.log(jnp.expm1(dt)),
        'dn_out_g': 1.0 + nrm(ks[6], (DEPTH, DN_HEAD_DIM), 0.1),
        'sb_q_g': 1.0 + nrm(ks[7], (DEPTH, SB_HEAD_DIM), 0.1),
        'sb_k_g': 1.0 + nrm(ks[8], (DEPTH, SB_HEAD_DIM), 0.1),
        'sg_v_g': 1.0 + nrm(ks[9], (DEPTH, SG_WIDTH), 0.1),
        'sg_w': nrm(ks[10], (DEPTH, SG_GROUPS, SG_CHUNK, SG_CHUNK), SG_CHUNK ** -0.5),
        'sg_b': 1.0 + nrm(ks[11], (DEPTH, SG_GROUPS, SG_CHUNK), 0.1),
        'w_out': nrm(ks[12], (DEPTH, MIX_WIDTH, D_MODEL), MIX_WIDTH ** -0.5),
        'norm2_g': 1.0 + nrm(ks[13], (DEPTH, D_MODEL), 0.1),
        'w_ff1': nrm(ks[14], (DEPTH, D_MODEL, D_FF), D_MODEL ** -0.5),
        'w_ff2': nrm(ks[15], (DEPTH, D_FF, D_MODEL), D_FF ** -0.5),
    }


def reference(x, norm1_g, w_in, conv_w, a_log, dt_bias, dn_out_g, sb_q_g, sb_k_g, sg_v_g, sg_w, sg_b,
              w_out, norm2_g, w_ff1, w_ff2):
    for l in range(DEPTH):
        h = rms_norm(x, norm1_g[l])
        x = x + hybrid_mixer(h, w_in[l], conv_w[l], a_log[l], dt_bias[l], dn_out_g[l], sb_q_g[l], sb_k_g[l],
                             sg_v_g[l], sg_w[l], sg_b[l], w_out[l])
        h = rms_norm(x, norm2_g[l])
        x = x + jnp.square(jax.nn.relu(h @ w_ff1[l])) @ w_ff2[l]
    return x
```

```python
import numpy as np
import concourse.bass as bass
import concourse.mybir as mybir
from concourse.bass_utils import run_bass_kernel_spmd
from contextlib import ExitStack

F32 = mybir.dt.float32
BF16 = mybir.dt.bfloat16
AF = mybir.ActivationFunctionType
ALU = mybir.AluOpType
AX = mybir.AxisListType

EPS = 1e-6
N_CORES = 8


class Buf:
    __slots__ = ("name", "w", "r", "excl", "dsem", "dcnt")

    def __init__(self, name, excl=False):
        self.name = name
        self.w = None
        self.r = []
        self.excl = excl
        self.dsem = None
        self.dcnt = 0


class Eng:
    def __init__(self, name, sem):
        self.name = name
        self.sem = sem
        self.cnt = 0
        self.known = {}
        self.prog = []
        self.labels = []


class KB:
    def __init__(self, nc, stack):
        self.nc = nc
        self.stack = stack
        self.sems = {}
        self.eng = {}
        self.dbufs = []
        self.stage = "pre"
        self.capture = None
        for n in ("pe", "act", "dve", "pool", "sp"):
            s = stack.enter_context(nc.semaphore("s_" + n))
            self.sems["e_" + n] = s
            self.eng[n] = Eng(n, "e_" + n)

    def _deps(self, e, reads, writes):
        deps = {}

        def add(ev):
            if ev is None:
                return
            k, v, en = ev
            if en == "pe" and e.name == "pe":
                return
            if deps.get(k, 0) < v:
                deps[k] = v

        for b in reads:
            add(b.w)
            if b.excl:
                for ev in b.r:
                    if ev[2] != e.name:
                        add(ev)
        for b in writes:
            add(b.w)
            for ev in b.r:
                add(ev)
        waits = []
        for k, v in deps.items():
            if e.known.get(k, 0) >= v:
                continue
            e.known[k] = v
            waits.append((k, v))
        return waits

    def op(self, en, fns, reads=(), writes=()):
        if self.capture is not None:
            self.capture.append((en, fns, tuple(reads), tuple(writes), self.stage))
            return
        e = self.eng[en]
        if callable(fns):
            fns = [fns]
        waits = self._deps(e, reads, writes)
        e.cnt += 1
        ev = (e.sem, e.cnt, e.name)
        for i, fn in enumerate(fns):
            last = i == len(fns) - 1
            e.prog.append((waits if i == 0 else [], fn, (e.sem, 1) if last else None))
            e.labels.append(self.stage)
        for b in reads:
            b.r.append(ev)
        for b in writes:
            b.w = ev
            b.r = []

    def cap_begin(self):
        self.capture = []

    def cap_end(self):
        c, self.capture = self.capture, None
        return c

    def emit(self, rec):
        en, fns, r, w, st = rec
        self.stage = st
        self.op(en, fns, r, w)

    def dma(self, en, out, in_, dbuf, reads=(), writes=()):
        e = self.eng[en]
        if dbuf.dsem is None:
            key = "d_%d" % len(self.sems)
            self.sems[key] = self.stack.enter_context(self.nc.semaphore(key))
            dbuf.dsem = key
            self.dbufs.append(dbuf)
        waits = self._deps(e, reads, writes)
        dbuf.dcnt += 16
        ev = (dbuf.dsem, dbuf.dcnt, "dma")
        e.prog.append((waits, (lambda h, o=out, i=in_: h.dma_start(out=o, in_=i)), (dbuf.dsem, 16)))
        e.labels.append("dma:" + self.stage)
        for b in reads:
            b.r.append(ev)
        for b in writes:
            b.w = ev
            b.r = []

    def barrier(self):
        evs = [(e.sem, e.cnt) for e in self.eng.values() if e.cnt > 0]
        evs += [(b.dsem, b.dcnt) for b in self.dbufs]
        for e in self.eng.values():
            waits = []
            for k, v in evs:
                if e.known.get(k, 0) < v:
                    e.known[k] = v
                    waits.append((k, v))
            if waits:
                e.prog.append((waits, None, None))

    def finish(self, block):
        self.barrier()
        sems = self.sems
        engs = self.eng

        def replay(h, name):
            for waits, fn, inc in engs[name].prog:
                for k, v in waits:
                    h.wait_ge(sems[k], v)
                if fn is None:
                    continue
                ins = fn(h)
                if inc is not None:
                    ins.then_inc(sems[inc[0]], inc[1])

        @block.tensor
        def _(h):
            replay(h, "pe")

        @block.scalar
        def _(h):
            replay(h, "act")

        @block.vector
        def _(h):
            replay(h, "dve")

        @block.gpsimd
        def _(h):
            replay(h, "pool")

        @block.sync
        def _(h):
            replay(h, "sp")


class T:
    __slots__ = ("ap", "b")

    def __init__(self, ap, b):
        self.ap = ap
        self.b = b


C_ID, C_TRI, C_BLK, C_MI, C_MS, C_MIT, C_MW, C_NTRI, C_NH, NCST = 0, 128, 256, 384, 512, 640, 768, 1664, 1792, 1800
P_G1, P_G2, P_CONV, P_ALOG, P_DTB, P_DNG, P_SBQ, P_SBK, P_SGB, P_SGVG, P_SGWT, NPL = 0, 8, 16, 64, 68, 72, 73, 74, 76, 80, 336, 848


def make_consts():
    c = np.zeros((128, NCST), np.float32)
    i = np.arange(128)
    c[:, C_ID:C_ID + 128] = np.eye(128)
    same = (i[:, None] // 64) == (i[None, :] // 64)
    c[:, C_TRI:C_TRI + 128] = same & (i[:, None] <= i[None, :])
    c[:, C_BLK:C_BLK + 128] = same
    c[:, C_MI:C_MI + 128] = same & (i[None, :] <= i[:, None])
    c[:, C_MS:C_MS + 128] = same & (i[None, :] < i[:, None])
    c[:, C_MIT:C_MIT + 128] = i[None, :] >= i[:, None]
    u = np.arange(896)
    c[:, C_MW:C_MW + 896] = (u[None, :] - 384) > i[:, None]
    c[:, C_NTRI:C_NTRI + 128] = -1.0 * (i[:, None] > i[None, :])
    c[:, C_NH] = -0.5
    return c


def make_params(inp, L):
    pl = np.zeros((L, 128, NPL), np.float32)
    for l in range(L):
        pl[l, :, P_G1:P_G1 + 8] = inp["norm1_g"][l].reshape(8, 128).T
        pl[l, :, P_G2:P_G2 + 8] = inp["norm2_g"][l].reshape(8, 128).T
        pl[l, :, P_CONV:P_CONV + 48] = inp["conv_w"][l].reshape(4, 12, 128).transpose(2, 1, 0).reshape(128, 48)
        pl[l, :, P_ALOG:P_ALOG + 4] = inp["a_log"][l][None, :]
        pl[l, :, P_DTB:P_DTB + 4] = inp["dt_bias"][l][None, :]
        pl[l, :, P_DNG] = inp["dn_out_g"][l]
        pl[l, :, P_SBQ] = np.tile(inp["sb_q_g"][l], 2)
        pl[l, :, P_SBK] = np.tile(inp["sb_k_g"][l], 2)
        pl[l, :, P_SGB:P_SGB + 4] = inp["sg_b"][l].T
        pl[l, :, P_SGVG:P_SGVG + 256] = inp["sg_v_g"][l][None, :]
        pl[l, :, P_SGWT:P_SGWT + 512] = inp["sg_w"][l].transpose(2, 0, 1).reshape(128, 512)
    return pl


WIN_SLICES = [
    [(0, 512)], [(512, 512)], [(1024, 512)], [(1536, 512)],
    [(2056, 512)],
    [(2568, 256), (2048, 8)],
    [(2824, 512)],
]


WIN_ORDER = [0, 1, 2, 3, 5, 6, 4]
JUNK = True


def build(NSEQ, S, L, dbg=False):
    NCH = S // 512
    NT = S // 128
    nc = bass.Bass("TRN2", target_bir_lowering=False)
    x_d = nc.dram_tensor("x", [NSEQ, S, 1024], F32, kind="ExternalInput").ap()
    win_d = nc.dram_tensor("w_in", [L, 1024, 3336], F32, kind="ExternalInput").ap()
    wout_d = nc.dram_tensor("w_out", [L, 1024, 1024], F32, kind="ExternalInput").ap()
    wf1_d = nc.dram_tensor("w_ff1", [L, 1024, 4096], F32, kind="ExternalInput").ap()
    wf2_d = nc.dram_tensor("w_ff2", [L, 4096, 1024], F32, kind="ExternalInput").ap()
    pl_d = nc.dram_tensor("pl", [L, 128, NPL], F32, kind="ExternalInput").ap()
    cst_d = nc.dram_tensor("cst", [128, NCST], F32, kind="ExternalInput").ap()
    y_d = nc.dram_tensor("y", [NSEQ, S, 1024], F32, kind="ExternalOutput").ap()
    dbg_d = nc.dram_tensor("dbg", [128, 8, 512], F32, kind="ExternalOutput").ap() if dbg else None

    st = ExitStack()
    with st:
        kb = KB(nc, st)
        MEMB = 212480
        M = st.enter_context(nc.sbuf_tensor("M", [128, MEMB // 4], F32))

        class Arena:
            def __init__(self, base, limit):
                self.off = base
                self.limit = limit

            def a(self, name, nelem, dtype, shape=None):
                nbytes = nelem * (4 if dtype == F32 else 2)
                nbytes = (nbytes + 63) // 64 * 64
                o = self.off
                self.off += nbytes
                assert self.off <= self.limit, (name, self.off, self.limit)
                ap = M[:, o // 4:(o + nbytes) // 4]
                if dtype == BF16:
                    ap = ap.bitcast(BF16)
                ap = ap[:, 0:nelem]
                if shape is not None:
                    if len(shape) == 2:
                        ap = ap.rearrange("p (a b) -> p a b", a=shape[0])
                    elif len(shape) == 3:
                        ap = ap.rearrange("p (a b c) -> p a b c", a=shape[0], b=shape[1])
                return T(ap, Buf(name))

        fixed = Arena(0, MEMB)
        xT = fixed.a("xT", 8 * S, F32, (8, S))
        NSLOT = 5
        slots = [fixed.a("slot%d" % i, 8 * 512, BF16, (8, 512)) for i in range(NSLOT)]
        cst = fixed.a("cst", NCST, F32)
        idb = fixed.a("idb", 128, BF16)
        oneb = fixed.a("oneb", 128, BF16)
        blkb = fixed.a("blkb", 128, BF16)
        ntrib = fixed.a("ntrib", 128, BF16)
        noneb = fixed.a("noneb", 128, BF16)
        nmask = fixed.a("nmask", 896, BF16)
        SCR0 = fixed.off

        PS = []
        for i in range(7):
            t = st.enter_context(nc.psum_tensor("ps%d" % i, [128, 512], F32))
            PS.append(T(t[:], Buf("ps%d" % i, excl=True)))
        tpb = st.enter_context(nc.psum_tensor("psb", [128, 1024], BF16))
        PB = T(tpb[:], Buf("psb", excl=True))

        def cs(off, n=128):
            return cst.ap[:, off:off + n]

        ident = cs(C_ID)
        def act(out, in_, func, r, w, **kw):
            kb.op("act", lambda h: h.activation(out=out, in_=in_, func=func, **kw), reads=r, writes=w)

        def tt(en, out, in0, in1, op, r, w):
            kb.op(en, lambda h: h.tensor_tensor(out=out, in0=in0, in1=in1, op=op), reads=r, writes=w)

        def ts(en, out, in0, s1, s2, op0, op1, r, w):
            if s2 is None:
                kb.op(en, lambda h: h.tensor_scalar(out=out, in0=in0, scalar1=s1, scalar2=None, op0=op0), reads=r, writes=w)
            else:
                kb.op(en, lambda h: h.tensor_scalar(out=out, in0=in0, scalar1=s1, scalar2=s2, op0=op0, op1=op1), reads=r, writes=w)

        def stt(out, in0, sc, in1, op0, op1, r, w):
            kb.op("dve", lambda h: h.scalar_tensor_tensor(out=out, in0=in0, scalar=sc, in1=in1, op0=op0, op1=op1), reads=r, writes=w)

        def cp(en, out, in_, r, w):
            if en == "act":
                act(out, in_, AF.Copy, r, w)
            else:
                kb.op(en, lambda h: h.tensor_copy(out=out, in_=in_), reads=r, writes=w)

        def mm(specs, r, w, sgc=False):
            if sgc:
                fns = [(lambda h, o=o, l=l, rr=rr, s=s, e=e: h.matmul(o, lhsT=l, rhs=rr, start=s, stop=e, skip_group_check=True)) for (o, l, rr, s, e) in specs]
            else:
                fns = [(lambda h, o=o, l=l, rr=rr, s=s, e=e: h.matmul(o, lhsT=l, rhs=rr, start=s, stop=e)) for (o, l, rr, s, e) in specs]
            kb.op("pe", fns, reads=r, writes=w)
            junk()

        def tr(specs, r, w):
            fns = [(lambda h, o=o, i=i, d=d: h.transpose(o, i, d)) for (o, i, d) in specs]
            kb.op("pe", fns, reads=r, writes=w)
            junk()

        jstate = {"bank": None, "n": 0}

        def junk():
            jb = jstate["bank"]
            if not JUNK or jb is None or jstate["n"] <= 0:
                return
            fns = [(lambda h, o=jb.ap: h.matmul(o, lhsT=idb.ap, rhs=nmask.ap[:, 0:512], start=True, stop=True)) for _ in range(jstate["n"])]
            kb.op("pe", fns, reads=[idb.b, nmask.b], writes=[jb.b])

        def rsqrt_psum(out_t, ps_t, addc, tmp_t, cols=512, scale=1.0):
            act(tmp_t.ap[:, 0:cols], ps_t.ap[:, 0:cols], AF.Ln, [ps_t.b], [tmp_t.b], bias=addc, scale=scale)
            act(out_t.ap[:, 0:cols], tmp_t.ap[:, 0:cols], AF.Exp, [tmp_t.b], [out_t.b], scale=-0.5)

        kb.dma("sp", cst.ap, cst_d, cst.b, writes=[cst.b])
        cp("dve", idb.ap, ident, [cst.b], [idb.b])
        cp("dve", blkb.ap, cs(C_BLK), [cst.b], [blkb.b])
        cp("dve", ntrib.ap, cs(C_NTRI), [cst.b], [ntrib.b])
        kb.op("pool", lambda h: h.memset(oneb.ap, 1.0), writes=[oneb.b])
        kb.op("pool", lambda h: h.memset(noneb.ap, -1.0), writes=[noneb.b])
        ts("dve", nmask.ap, cst.ap[:, C_MW:C_MW + 896], -1.0, 30000.0, ALU.add, ALU.mult, [cst.b], [nmask.b])

        wsc = {}

        def conv_block(key, ncols, pieces):
            th = nc.dram_tensor("wsc_%s" % "_".join(str(k) for k in key), [1024, ncols], BF16)
            ap = th.ap()
            b = Buf("wsc")
            c0 = 0
            for (src, n) in pieces:
                kb.dma("pool", ap[:, c0:c0 + n], src, b, writes=[b])
                c0 += n
            wsc[key] = (ap.rearrange("(k p) n -> p k n", p=128), b, ncols)

        conv_pending = []
        for l in range(L):
            for si, pieces in enumerate(WIN_SLICES):
                ncols = sum(n for _, n in pieces)
                conv_pending.append(((l, "in", si), ncols, [(win_d[l, :, c0:c0 + n], n) for (c0, n) in pieces]))
            for cg in range(2):
                conv_pending.append(((l, "out", cg), 512, [(wout_d[l, :, cg * 512:(cg + 1) * 512], 512)]))
            for cg in range(8):
                conv_pending.append(((l, "f1", cg), 512, [(wf1_d[l, :, cg * 512:(cg + 1) * 512], 512)]))
            for rg in range(4):
                for cg in range(2):
                    conv_pending.append(((l, "f2", rg, cg), 512, [(wf2_d[l, rg * 1024:(rg + 1) * 1024, cg * 512:(cg + 1) * 512], 512)]))

        def conv_next(n=1):
            for _ in range(n):
                if conv_pending:
                    conv_block(*conv_pending.pop(0))

        def conv_upto(l, name):
            idx = [i for i, (k, _, _) in enumerate(conv_pending) if k[0] == l and k[1] == name]
            if idx:
                conv_next(idx[-1] + 1)

        conv_upto(0, "out")

        sched = []
        for s in range(NSEQ):
            for l in range(L):
                for c in range(NCH):
                    for si in WIN_ORDER:
                        sched.append((l, "in", si))
                    if c == 0:
                        sched.append((l, "out", 0))
                        sched.append((l, "out", 1))
                for fg in range(4):
                    sched += [(l, "f1", 2 * fg), (l, "f1", 2 * fg + 1), (l, "f2", fg, 0), (l, "f2", fg, 1)]
        ws_state = {"pos": 0, "use": 0, "free": list(range(NSLOT)), "loaded": {}}

        def ws_fill():
            while ws_state["free"] and ws_state["pos"] < len(sched):
                sl = ws_state["free"].pop(0)
                key = sched[ws_state["pos"]]
                src, sb, ncols = wsc[key]
                kb.dma("sp", slots[sl].ap[:, :, 0:ncols], src, slots[sl].b, reads=[sb], writes=[slots[sl].b])
                ws_state["loaded"][ws_state["pos"]] = sl
                ws_state["pos"] += 1

        def ws_next(key):
            i = ws_state["use"]
            assert sched[i] == key, (sched[i], key)
            ws_state["use"] += 1
            assert i in ws_state["loaded"], "slice not prefetched (deadlock): %s" % (key,)
            return i, slots[ws_state["loaded"][i]]

        def ws_release(i):
            ws_state["free"].append(ws_state["loaded"].pop(i))
            ws_fill()

        for seq in range(NSEQ):
            kb.stage = "xload"
            kb.barrier()
            ar = Arena(SCR0, MEMB)
            xin = [ar.a("xin%d" % i, 1024, F32) for i in range(2)]
            for t in range(NT):
                xi = xin[t % 2]
                kb.dma("sp", xi.ap, x_d[seq, t * 128:(t + 1) * 128, :], xi.b, writes=[xi.b])
                for hf in range(2):
                    ps = PS[(2 * t + hf) % 4]
                    tr([(ps.ap[:, j * 128:(j + 1) * 128], xi.ap[:, (hf * 4 + j) * 128:(hf * 4 + j + 1) * 128], ident) for j in range(4)],
                       [xi.b, cst.b], [ps.b])
                    cp("act" if hf else "dve", xT.ap[:, hf * 4:hf * 4 + 4, t * 128:(t + 1) * 128],
                       ps.ap.rearrange("p (a b) -> p a b", a=4), [ps.b], [xT.b])

            if seq == 0:
                ws_fill()
            for l in range(L):
                kb.stage = "Asetup"
                kb.barrier()
                ar = Arena(SCR0, MEMB)
                pl = ar.a("pl", NPL, F32)
                drv = ar.a("drv", 64, F32)
                sgwm = ar.a("sgwm", 512, BF16, (4, 128))
                diag = [ar.a("diag%d" % i, 512, BF16, (4, 128)) for i in range(2)]
                hT = ar.a("hT", 8 * 512, BF16, (8, 512))
                oT = hT
                dq = ar.a("dq", 4 * 512, BF16, (4, 512))
                dk = ar.a("dk", 4 * 512, BF16, (4, 512))
                dv = ar.a("dv", 4 * 512, BF16, (4, 512))
                dz = ar.a("dz", 4 * 512, BF16, (4, 512))
                sbq = ar.a("sbq", 2 * 512, BF16, (2, 512))
                sbk = ar.a("sbk", 2 * S, BF16, (2, S))
                sbv = ar.a("sbv", NT * 256, BF16, (NT, 256))
                sgu = ar.a("sgu", 4 * 256, BF16, (4, 256))
                sgv = ar.a("sgv", 4 * 256, BF16, (4, 256))
                oraw = ar.a("oraw", 4 * 512, BF16, (4, 512))
                pre = [ar.a("pre%d" % i, 520, BF16) for i in range(2)]
                halo = ar.a("halo", 12 * 4, BF16, (12, 4))
                S32 = ar.a("S32", 512, F32, (4, 128))
                Sbfs = [ar.a("Sbf%d" % i, 512, BF16, (4, 128)) for i in range(2)]
                sm = ar.a("sm", 16 * 16, F32, (16, 16))
                gg = ar.a("gg", 32, F32, (4, 8))
                glb = ar.a("glb", 32, F32)
                T32 = [ar.a("t32_%d" % i, 512, F32) for i in range(8)]
                T16 = [ar.a("t16_%d" % i, 512, BF16) for i in range(6)]
                _su = sgu.ap.rearrange("p a b -> p (a b)")
                _sv = sgv.ap.rearrange("p a b -> p (a b)")
                T16b = [T(_su[:, 0:512], sgu.b), T(_su[:, 512:1024], sgu.b), T(_sv[:, 0:512], sgv.b), T(_sv[:, 512:1024], sgv.b)]
                Upar = [ar.a("Upar%d" % i, 512, F32) for i in range(2)]
                SdT = ar.a("SdT", 512, F32)
                glb2 = ar.a("glb2", 32, F32)

                kb.dma("sp", pl.ap, pl_d[l], pl.b, writes=[pl.b])
                ts("dve", drv.ap[:, 0:16], pl.ap[:, P_G1:P_G1 + 16], 32.0, None, ALU.mult, None, [pl.b], [drv.b])
                act(drv.ap[:, 16:20], pl.ap[:, P_ALOG:P_ALOG + 4], AF.Exp, [pl.b], [drv.b])
                ts("dve", drv.ap[:, 16:20], drv.ap[:, 16:20], -1.0, None, ALU.mult, None, [drv.b], [drv.b])
                ts("dve", drv.ap[:, 20:21], pl.ap[:, P_DNG:P_DNG + 1], 0.5, None, ALU.mult, None, [pl.b], [drv.b])
                ts("dve", drv.ap[:, 21:22], pl.ap[:, P_SBQ:P_SBQ + 1], 1.0, None, ALU.mult, None, [pl.b], [drv.b])
                ts("dve", drv.ap[:, 22:23], pl.ap[:, P_SBK:P_SBK + 1], 8.0, None, ALU.mult, None, [pl.b], [drv.b])
                ts("dve", drv.ap[:, 24:28], pl.ap[:, P_SGB:P_SGB + 4], 0.5, None, ALU.mult, None, [pl.b], [drv.b])
                stt(sgwm.ap, pl.ap[:, P_SGWT:P_SGWT + 512].rearrange("p (g t) -> p g t", g=4), 0.5,
                    cs(C_MIT).unsqueeze(1).to_broadcast([128, 4, 128]), ALU.mult, ALU.mult, [pl.b, cst.b], [sgwm.b])
                kb.op("pool", lambda h: h.memset(S32.ap, 0.0), writes=[S32.b])
                kb.op("pool", lambda h: h.memset(Sbfs[0].ap, 0.0), writes=[Sbfs[0].b])
                kb.op("pool", lambda h: h.memset(halo.ap, 0.0), writes=[halo.b])

                wout_i = [None, None]
                wout_t = [None, None]

                for c in range(NCH):
                    c0 = c * 512
                    conv_next(4 if NCH >= 4 else 16)
                    if c == NCH - 1:
                        conv_upto(l, "f2")
                    kb.stage = "A1norm"
                    def rmsnorm_chunk(gcol, dst, c0=c0):
                        act(dst.ap, xT.ap[:, :, c0:c0 + 512], AF.Square, [xT.b], [dst.b])
                        mm([(PS[0].ap, oneb.ap, dst.ap[:, k, :], k == 0, k == 7) for k in range(8)], [oneb.b, dst.b], [PS[0].b])
                        rsqrt_psum(T32[0], PS[0], 1024.0 * EPS, T32[1])
                        for k in range(8):
                            stt(dst.ap[:, k, :], xT.ap[:, k, c0:c0 + 512], drv.ap[:, gcol + k:gcol + k + 1], T32[0].ap,
                                ALU.mult, ALU.mult, [xT.b, drv.b, T32[0].b], [dst.b])
                    rmsnorm_chunk(0, hT)

                    kb.stage = "A2dnproj"
                    def proj_fm(slot, j, ps):
                        mm([(ps.ap, slot.ap[:, k, j * 128:(j + 1) * 128], hT.ap[:, k, :], k == 0, k == 7) for k in range(8)],
                           [slot.b, hT.b], [ps.b])

                    items = [(grp, hh) for grp in range(4) for hh in range(4)]
                    dsts = (dq, dk, dv, dz)
                    cur_w = {}

                    def emit_proj(grp, hh):
                        if hh == 0:
                            cur_w["w"] = ws_next((l, "in", grp))
                        wi, wsl = cur_w["w"]
                        j = grp * 4 + hh
                        ps = PS[1 + (j % 2)]
                        proj_fm(wsl, hh, ps)
                        if hh == 3:
                            ws_release(wi)
                        if grp < 3:
                            pr = pre[j % 2]
                            dg = diag[j % 2]
                            cp("pool", pr.ap[:, 0:3], halo.ap[:, j, 0:3], [halo.b], [pr.b])
                            cp("act", pr.ap[:, 3:515], ps.ap, [ps.b], [pr.b])
                            cp("pool", halo.ap[:, j, 0:3], pr.ap[:, 512:515], [pr.b], [halo.b])
                            for i in range(4):
                                act(dg.ap[:, i, :], idb.ap, AF.Copy, [idb.b, pl.b], [dg.b], scale=pl.ap[:, P_CONV + j * 4 + i:P_CONV + j * 4 + i + 1])

                    def emit_conv(grp, hh):
                        j = grp * 4 + hh
                        dst = dsts[grp]
                        if grp < 3:
                            pr = pre[j % 2]
                            dg = diag[j % 2]
                            pc = PS[3 + (j % 2)]
                            mm([(pc.ap, dg.ap[:, i, :], pr.ap[:, i:i + 512], i == 0, i == 3) for i in range(4)], [dg.b, pr.b], [pc.b])
                        else:
                            pc = PS[1 + (j % 2)]
                        th = T32[2 + (j % 2)]
                        act(th.ap, pc.ap, AF.Tanh, [pc.b], [th.b], scale=0.5)
                        stt(dst.ap[:, hh, :], th.ap, 1.0, pc.ap, ALU.add, ALU.mult, [th.b, pc.b], [dst.b])

                    prev = None
                    for it in items + [None]:
                        if it is not None:
                            emit_proj(*it)
                        if prev is not None:
                            emit_conv(*prev)
                        prev = it
                    kb.stage = "A3tok"
                    wi5, w5 = ws_next((l, "in", 5))
                    wi6, w6 = ws_next((l, "in", 6))
                    ab = sm.ap[:, 0:2, :].rearrange("p a (t e) -> p (a t) e", e=8)
                    def a3x(t4):
                        tg = c * 4 + t4
                        tc = slice(t4 * 128, (t4 + 1) * 128)
                        pa = PS[1 + (t4 % 2)]
                        mm([(pa.ap[:, 0:264], hT.ap[:, k, tc], w5.ap[:, k, 0:264], k == 0, k == 7) for k in range(8)], [hT.b, w5.b], [pa.b])
                        cp("act", sbv.ap[:, tg, :], pa.ap[:, 0:256], [pa.b], [sbv.b])
                        cp("dve", ab[:, t4, :], pa.ap[:, 256:264], [pa.b], [sm.b])
                        pg = PS[3 + (t4 % 2)]
                        mm([(pg.ap, hT.ap[:, k, tc], w6.ap[:, k, :], k == 0, k == 7) for k in range(8)], [hT.b, w6.b], [pg.b])
                        xs, t1, t2 = T32[0 + 3 * (t4 % 2)], T32[1 + 3 * (t4 % 2)], T32[2 + 3 * (t4 % 2)]
                        cp("act", xs.ap, pg.ap, [pg.b], [xs.b])
                        act(t1.ap, pg.ap, AF.Square, [pg.b], [t1.b])

                    def a3y(t4):
                        xs, t1, t2 = T32[0 + 3 * (t4 % 2)], T32[1 + 3 * (t4 % 2)], T32[2 + 3 * (t4 % 2)]
                        ts("dve", t1.ap, t1.ap, 0.044715, 1.0, ALU.mult, ALU.add, [t1.b], [t1.b])
                        tt("dve", t1.ap, t1.ap, xs.ap, ALU.mult, [t1.b, xs.b], [t1.b])
                        act(t2.ap, t1.ap, AF.Tanh, [t1.b], [t2.b], scale=0.7978845608028654)
                        stt(sgu.ap[:, t4, :], t2.ap[:, 0:256], 1.0, xs.ap[:, 0:256], ALU.add, ALU.mult, [t2.b, xs.b], [sgu.b])
                        stt(t1.ap[:, 256:512], t2.ap[:, 256:512], 1.0, xs.ap[:, 256:512], ALU.add, ALU.mult, [t2.b, xs.b], [t1.b])
                        act(t2.ap[:, 256:512], t1.ap[:, 256:512], AF.Square, [t1.b], [t2.b])
                        ssr = sm.ap[:, 2 + (t4 % 2), 0:4]
                        kb.op("dve", lambda h, o=ssr, i=t2.ap[:, 256:512].rearrange("p (g d) -> p g d", g=4): h.tensor_reduce(out=o, in_=i, axis=AX.X, op=ALU.add),
                              reads=[t2.b], writes=[sm.b])
                        ts("dve", ssr, ssr, 1.0 / 64.0, 4.0 * EPS, ALU.mult, ALU.add, [sm.b], [sm.b])
                        tt("pool", ssr, ssr, cst.ap[:, C_NH:C_NH + 1].to_broadcast([128, 4]), ALU.pow, [sm.b, cst.b], [sm.b])
                        tt("dve", t1.ap[:, 256:512].rearrange("p (g d) -> p g d", g=4), t1.ap[:, 256:512].rearrange("p (g d) -> p g d", g=4),
                           ssr.unsqueeze(2).to_broadcast([128, 4, 64]), ALU.mult, [t1.b, sm.b], [t1.b])
                        tt("pool", sgv.ap[:, t4, :], t1.ap[:, 256:512], pl.ap[:, P_SGVG:P_SGVG + 256], ALU.mult, [t1.b, pl.b], [sgv.b])

                    for t4 in range(5):
                        if t4 < 4:
                            a3x(t4)
                        if t4 >= 1:
                            a3y(t4 - 1)
                    ws_release(wi5)
                    ws_release(wi6)

                    kb.stage = "l2n_sbproj"
                    l2items = [("dn", grp, hh) for grp in range(2) for hh in range(4)] + [("sb", j, 0) for j in range(4)]
                    sbw = {}

                    PNB = (PS[5], PS[6], PS[3], PS[4])

                    def stX(n, kind, a_, b_):
                        sqb = T16[n % 4]
                        pn = PNB[n % 4]
                        if kind == "dn":
                            dst = (dq, dk)[a_]
                            tt("dve", sqb.ap, dst.ap[:, b_, :], dst.ap[:, b_, :], ALU.mult, [dst.b], [sqb.b])
                            mm([(pn.ap, oneb.ap, sqb.ap, True, True)], [oneb.b, sqb.b], [pn.b])
                        else:
                            j = a_
                            if j == 0:
                                sbw["w"] = ws_next((l, "in", 4))
                            wi, wsl = sbw["w"]
                            ps = PS[1 + (j % 2)]
                            proj_fm(wsl, j, ps)
                            if j == 3:
                                ws_release(wi)
                            xs = T32[(j % 4)]
                            cp("act", xs.ap, ps.ap, [ps.b], [xs.b])
                            tt("dve", sqb.ap, xs.ap, xs.ap, ALU.mult, [xs.b], [sqb.b])
                            mm([(pn.ap, blkb.ap, sqb.ap, True, True)], [blkb.b, sqb.b], [pn.b])

                    def stY(n, kind, a_, b_):
                        pn = PNB[n % 4]
                        rs = T32[4 + (n % 4)]
                        if kind == "dn":
                            dst = (dq, dk)[a_]
                            rsqrt_psum(rs, pn, 4.0 * EPS, rs)
                            if a_ == 0:
                                stt(dst.ap[:, b_, :], dst.ap[:, b_, :], 128.0 ** -0.5, rs.ap, ALU.mult, ALU.mult, [dst.b, rs.b], [dst.b])
                            else:
                                tt("dve", dst.ap[:, b_, :], dst.ap[:, b_, :], rs.ap, ALU.mult, [dst.b, rs.b], [dst.b])
                        else:
                            j = a_
                            xs = T32[(j % 4)]
                            rsqrt_psum(rs, pn, 64.0 * EPS, rs)
                            if j < 2:
                                stt(sbq.ap[:, j, :], xs.ap, drv.ap[:, 21:22], rs.ap, ALU.mult, ALU.mult, [xs.b, drv.b, rs.b], [sbq.b])
                            else:
                                stt(sbk.ap[:, j - 2, c0:c0 + 512], xs.ap, drv.ap[:, 22:23], rs.ap, ALU.mult, ALU.mult, [xs.b, drv.b, rs.b], [sbk.b])

                    for n in range(len(l2items) + 2):
                        if n < len(l2items):
                            stX(n, *l2items[n])
                        if n >= 2:
                            stY(n - 2, *l2items[n - 2])

                    kb.stage = "SGmix"
                    for t4 in range(4):
                        pm = PS[1 + (t4 % 2)]
                        for g in range(4):
                            mm([(pm.ap[:, g * 64:(g + 1) * 64], sgwm.ap[:, g, :], sgv.ap[:, t4, g * 64:(g + 1) * 64], True, True)],
                               [sgwm.b, sgv.b], [pm.b])
                        og = T16[2 + (t4 % 2)]
                        for g in range(4):
                            stt(og.ap[:, g * 64:(g + 1) * 64], pm.ap[:, g * 64:(g + 1) * 64], drv.ap[:, 24 + g:25 + g], sgu.ap[:, t4, g * 64:(g + 1) * 64],
                                ALU.add, ALU.mult, [pm.b, drv.b, sgu.b], [og.b])
                        tr([(PB.ap[:, jj * 128:(jj + 1) * 128], og.ap[:, jj * 128:(jj + 1) * 128], idb.ap) for jj in range(2)], [og.b, idb.b], [PB.b])
                        cp("act", oT.ap[:, 6:8, t4 * 128:(t4 + 1) * 128], PB.ap[:, 0:256].rearrange("p (a b) -> p a b", a=2), [PB.b], [oT.b])

                    kb.stage = "DNprep"
                    a_in = ab[:, :, 0:4]
                    b_in = ab[:, :, 4:8]
                    def smr(i):
                        return sm.ap[:, i, :].rearrange("p (t e) -> p t e", e=4)
                    g_t, beta, nbeta, vb, eg, el, bq, tmpa, tmpb = [smr(i) for i in range(4, 13)]
                    dtb_b = pl.ap[:, P_DTB:P_DTB + 4].unsqueeze(1).to_broadcast([128, 4, 4])
                    nA_b = drv.ap[:, 16:20].unsqueeze(1).to_broadcast([128, 4, 4])
                    tt("dve", tmpa, a_in, dtb_b, ALU.add, [sm.b, pl.b], [sm.b])
                    act(tmpa, tmpa, AF.Exp, [sm.b], [sm.b])
                    act(tmpa, tmpa, AF.Ln, [sm.b], [sm.b], bias=1.0)
                    tt("dve", g_t, tmpa, nA_b, ALU.mult, [sm.b, drv.b], [sm.b])
                    act(tmpb, b_in, AF.Exp, [sm.b], [sm.b], scale=-1.0)
                    ts("dve", tmpb, tmpb, 1.0, None, ALU.add, None, [sm.b], [sm.b])
                    kb.op("dve", lambda h, o=beta, i=tmpb: h.reciprocal(out=o, in_=i), reads=[sm.b], writes=[sm.b])
                    ts("dve", nbeta, beta, -1.0, None, ALU.mult, None, [sm.b], [sm.b])
                    ts("dve", vb, beta, 0.5, None, ALU.mult, None, [sm.b], [sm.b])
                    pgc = PS[0]
                    specs = []
                    for t4 in range(4):
                        specs.append((pgc.ap[:, t4 * 8:t4 * 8 + 4], cs(C_TRI), g_t[:, t4, :], True, True))
                        specs.append((pgc.ap[:, t4 * 8 + 4:t4 * 8 + 8], cs(C_BLK), g_t[:, t4, :], True, True))
                    mm(specs, [cst.b, sm.b], [pgc.b])
                    cp("dve", gg.ap, pgc.ap[:, 0:32].rearrange("p (t e) -> p t e", e=8), [pgc.b], [gg.b])
                    gc = gg.ap[:, :, 0:4]
                    gl = gg.ap[:, :, 4:8]
                    act(eg, gc, AF.Exp, [gg.b], [sm.b])
                    tt("dve", tmpa, gl, gc, ALU.subtract, [gg.b], [sm.b])
                    act(el, tmpa, AF.Exp, [sm.b], [sm.b])
                    tt("dve", bq, beta, eg, ALU.mult, [sm.b], [sm.b])

                    def dn_prep(t4):
                        kb.stage = "DNtile"
                        par = t4 % 2
                        glbT = glb if par == 0 else glb2
                        tc = slice(t4 * 128, (t4 + 1) * 128)
                        Gb, A, B, C_, D_, E_, F_, X_ = T32

                        def v4(t):
                            return t.ap.rearrange("p (h n) -> p h n", h=4)

                        def bc4(ap):
                            return ap.unsqueeze(2).to_broadcast([128, 4, 128])

                        def cb4(off):
                            return cs(off).unsqueeze(1).to_broadcast([128, 4, 128])

                        def hsl(h):
                            return slice(h * 128, (h + 1) * 128)

                        cp("act", v4(Gb), bc4(g_t[:, t4, :]), [sm.b], [Gb.b])
                        mm([(PS[4].ap[:, hsl(h)], v4(Gb)[:, h, :], cs(C_TRI), True, True) for h in range(4)], [Gb.b, cst.b], [PS[4].b])
                        mm([(PS[2].ap[:, h * 2:h * 2 + 2], v4(Gb)[:, h, :], cst.ap[:, C_BLK:C_BLK + 128:64], True, True) for h in range(4)],
                           [Gb.b, cst.b], [PS[2].b])
                        act(glbT.ap[:, 0:8], PS[2].ap[:, 0:8], AF.Exp, [PS[2].b], [glbT.b])
                        for h in range(4):
                            ts("dve", A.ap[:, hsl(h)], PS[4].ap[:, hsl(h)], gc[:, t4, h:h + 1], 0.0, ALU.subtract, ALU.max, [PS[4].b, gg.b], [A.b])
                        act(B.ap, A.ap, AF.Exp, [A.b], [B.b], scale=-1.0)
                        tt("dve", v4(D_), v4(B), cb4(C_MS), ALU.mult, [B.b, cst.b], [D_.b])
                        tt("dve", v4(C_), v4(B), cb4(C_MI), ALU.mult, [B.b, cst.b], [C_.b])
                        act(A.ap, PS[4].ap, AF.Exp, [PS[4].b], [A.b])
                        qd = T16[0] if par == 0 else T16b[0]
                        tt("pool", v4(qd), dq.ap[:, :, tc], v4(A), ALU.mult, [dq.b, A.b], [qd.b])
                        mm([(PS[2].ap[:, hsl(h)], dk.ap[:, h, tc], dk.ap[:, h, tc], True, True) for h in range(4)], [dk.b], [PS[2].b])
                        mm([(PS[3].ap[:, hsl(h)], dq.ap[:, h, tc], dk.ap[:, h, tc], True, True) for h in range(4)], [dq.b, dk.b], [PS[3].b])
                        for h in range(4):
                            stt(E_.ap[:, hsl(h)], PS[2].ap[:, hsl(h)], nbeta[:, t4, h:h + 1], D_.ap[:, hsl(h)], ALU.mult, ALU.mult,
                                [PS[2].b, sm.b, D_.b], [E_.b])
                        qkm = T16[1]
                        tt("dve", qkm.ap, PS[3].ap, C_.ap, ALU.mult, [PS[3].b, C_.b], [qkm.b])
                        tr([(PS[4].ap[:, hsl(h)], v4(E_)[:, h, :], ident) for h in range(4)], [E_.b, cst.b], [PS[4].b])
                        cp("act", F_.ap, PS[4].ap, [PS[4].b], [F_.b])
                        Xs = X_.ap[:, 0:256].rearrange("p (h n) -> p h n", h=4)
                        for hb in range(2):
                            hs_ = slice(hb * 64, hb * 64 + 64)
                            tt("dve", Xs[hs_], PS[4].ap.rearrange("p (h n) -> p h n", h=4)[hs_, :, hb * 64:hb * 64 + 64],
                               ident[hs_, hb * 64:hb * 64 + 64].unsqueeze(1).to_broadcast([64, 4, 64]), ALU.add, [PS[4].b, cst.b], [X_.b])
                        tr([(PB.ap[:, hsl(h)], dk.ap[:, h, tc], idb.ap) for h in range(4)], [dk.b, idb.b], [PB.b])
                        Kbe = A
                        tt("dve", v4(Kbe), PB.ap[:, 0:512].rearrange("p (h n) -> p h n", h=4), bc4(bq[:, t4, :]), ALU.mult, [PB.b, sm.b], [Kbe.b])
                        kdec = T16[3] if par == 0 else T16b[1]
                        tt("dve", v4(kdec), PB.ap[:, 0:512].rearrange("p (h n) -> p h n", h=4), bc4(el[:, t4, :]), ALU.mult, [PB.b, sm.b], [kdec.b])
                        tr([(PB.ap[:, 512 + h * 128:512 + (h + 1) * 128], dv.ap[:, h, tc], idb.ap) for h in range(4)], [dv.b, idb.b], [PB.b])
                        Vb = B
                        tt("dve", v4(Vb), PB.ap[:, 512:1024].rearrange("p (h n) -> p h n", h=4), bc4(vb[:, t4, :]), ALU.mult, [PB.b, sm.b], [Vb.b])
                        tr([(PB.ap[:, hsl(h)], v4(qkm)[:, h, :], idb.ap) for h in range(4)], [qkm.b, idb.b], [PB.b])
                        qkT = T16[4] if par == 0 else T16b[2]
                        cp("act", qkT.ap, PB.ap[:, 0:512], [PB.b], [qkT.b])
                        Qc, Pc = E_, F_
                        Qn, Pn = C_, D_
                        for lev in range(1, 6):
                            mm([(PS[2].ap[:, hsl(h)], v4(Pc)[:, h, :], v4(Qc)[:, h, :], True, True) for h in range(4)], [Pc.b, Qc.b], [PS[2].b])
                            if lev < 5:
                                mm([(PS[3].ap[:, hsl(h)], v4(Qc)[:, h, :], v4(Pc)[:, h, :], True, True) for h in range(4)], [Pc.b, Qc.b], [PS[3].b])
                            cp("act", Qn.ap, PS[2].ap, [PS[2].b], [Qn.b])
                            if lev < 5:
                                cp("dve", Pn.ap, PS[3].ap, [PS[3].b], [Pn.b])
                            mm([(PS[4].ap[:, h * 64:(h + 1) * 64], v4(Qn)[:, h, :], Xs[:, h, :], True, True) for h in range(4)], [Qn.b, X_.b], [PS[4].b])
                            tt("dve", X_.ap[:, 0:256], X_.ap[:, 0:256], PS[4].ap[:, 0:256], ALU.add, [X_.b, PS[4].b], [X_.b])
                            Qc, Qn = Qn, Qc
                            Pc, Pn = Pn, Pc
                        Xbd = Pn
                        for hb in range(2):
                            act(v4(Xbd)[:, :, hb * 64:hb * 64 + 64], Xs, AF.Copy, [X_.b, cst.b], [Xbd.b], scale=cst.ap[:, C_BLK + hb * 64:C_BLK + hb * 64 + 1])
                        X_ = Xbd
                        mm([(PS[2].ap[:, hsl(h)], v4(Kbe)[:, h, :], v4(X_)[:, h, :], True, True) for h in range(4)], [Kbe.b, X_.b], [PS[2].b])
                        WT = T16[2] if par == 0 else T16b[3]
                        cp("act", WT.ap, PS[2].ap, [PS[2].b], [WT.b])
                        mm([(PS[3].ap[:, hsl(h)], v4(X_)[:, h, :], v4(Vb)[:, h, :], True, True) for h in range(4)], [X_.b, Vb.b], [PS[3].b])
                        U = Upar[par]
                        cp("act", U.ap, PS[3].ap, [PS[3].b], [U.b])
                        return dict(qd=qd, kdec=kdec, qkT=qkT, WT=WT, U=U, glbT=glbT)

                    def dn_scan(t4, tb):
                        qd, kdec, qkT, WT, U, glbT = tb['qd'], tb['kdec'], tb['qkT'], tb['WT'], tb['U'], tb['glbT']

                        def v4(t):
                            return t.ap.rearrange("p (h n) -> p h n", h=4)

                        def hsl(h):
                            return slice(h * 128, (h + 1) * 128)

                        for half in range(2):
                            kb.stage = "DNscan"
                            hs = slice(half * 64, half * 64 + 64)
                            tcs = slice(t4 * 128 + half * 64, t4 * 128 + half * 64 + 64)
                            Sbf, SbfN = Sbfs[half], Sbfs[1 - half]
                            mm([(PS[5].ap[:, hsl(h)], v4(WT)[:, h, :], Sbf.ap[:, h, :], True, True) for h in range(4)], [WT.b, Sbf.b], [PS[5].b])
                            un = T16[5]
                            tt("dve", un.ap[hs, :], U.ap[hs, :], PS[5].ap[hs, :], ALU.subtract, [U.b, PS[5].b], [un.b])
                            mm([(PS[0].ap[:, hsl(h)], v4(kdec)[hs, h, :], v4(un)[hs, h, :], True, True) for h in range(4)], [kdec.b, un.b], [PS[0].b])
                            specs = []
                            for h in range(4):
                                specs.append((PS[6].ap[:, h * 64:(h + 1) * 64], Sbf.ap[:, h, :], v4(qd)[:, h, hs], True, False))
                                specs.append((PS[6].ap[:, h * 64:(h + 1) * 64], v4(un)[hs, h, :], v4(qkT)[hs, h, hs], False, True))
                            mm(specs, [Sbf.b, qd.b, un.b, qkT.b], [PS[6].b])
                            cp("act", oraw.ap[:, :, tcs], PS[6].ap[:, 0:256].rearrange("p (h n) -> p h n", h=4), [PS[6].b], [oraw.b])
                            Sd = SdT
                            tt("pool", v4(Sd), S32.ap, glbT.ap[:, 0:8].rearrange("p (h f) -> p h f", f=2)[:, :, half:half + 1].to_broadcast([128, 4, 128]),
                               ALU.mult, [S32.b, glbT.b], [Sd.b])
                            tt("dve", SbfN.ap, v4(Sd), PS[0].ap.rearrange("p (h n) -> p h n", h=4), ALU.add, [Sd.b, PS[0].b], [SbfN.b])
                            tt("dve", S32.ap, v4(Sd), PS[0].ap.rearrange("p (h n) -> p h n", h=4), ALU.add, [Sd.b, PS[0].b], [S32.b])

                    preps, scans, tbs = [], [], []
                    for t4 in range(4):
                        kb.cap_begin()
                        tbs.append(dn_prep(t4))
                        preps.append(kb.cap_end())
                    for t4 in range(4):
                        kb.cap_begin()
                        dn_scan(t4, tbs[t4])
                        scans.append(kb.cap_end())
                    for rec in preps[0]:
                        kb.emit(rec)
                    for t4 in range(4):
                        A_, B_ = scans[t4], (preps[t4 + 1] if t4 < 3 else [])
                        ia = ib = 0
                        while ia < len(A_) or ib < len(B_):
                            if ib < len(B_) and (ia >= len(A_) or ib * len(A_) <= ia * len(B_)):
                                kb.emit(B_[ib])
                                ib += 1
                            else:
                                kb.emit(A_[ia])
                                ia += 1
                    jstate["bank"] = None
                    kb.stage = "DNout"
                    for h in range(4):
                        sqb = T16[h % 2]
                        act(sqb.ap, oraw.ap[:, h, :], AF.Square, [oraw.b], [sqb.b])
                        pn = PS[1 + (h % 2)]
                        mm([(pn.ap, oneb.ap, sqb.ap, True, True)], [oneb.b, sqb.b], [pn.b])
                        rs = T32[h % 2]
                        rsqrt_psum(rs, pn, EPS, rs, scale=1.0 / 128.0)
                        o2 = T32[2 + (h % 2)]
                        tt("pool", o2.ap, oraw.ap[:, h, :], rs.ap, ALU.mult, [oraw.b, rs.b], [o2.b])
                        stt(oT.ap[:, h, :], o2.ap, drv.ap[:, 20:21], dz.ap[:, h, :], ALU.mult, ALU.mult, [o2.b, drv.b, dz.b], [oT.b])

                    kb.stage = "ATTN"
                    wb2 = T(T32[7].ap.bitcast(BF16)[:, 0:512], T32[7].b)
                    acc = T32[0]
                    nb = 4 * c + 4
                    pairs = [(h, bi, b) for h in range(4) for bi, b in enumerate(range(nb - 1, -1, -1))]
                    NP_ = len(pairs)

                    def geo(n):
                        h, bi, b = pairs[n]
                        r = b - 4 * c
                        q0 = max(r, 0) * 128
                        qs = slice(q0, 512)
                        nm = nmask.ap[:, 384:896 - 128 * r] if r >= 0 else None
                        hp = slice((h % 2) * 64, (h % 2) * 64 + 64)
                        hc = h // 2
                        kT_ = sbk.ap[hp, hc, b * 128:(b + 1) * 128]
                        q_ = sbq.ap[hp, hc, qs]
                        return h, bi, b, r, q0, qs, nm, hp, hc, kT_, q_

                    def k0(n):
                        h, bi, b, r, q0, qs, nm, hp, hc, kT_, q_ = geo(n)
                        pz = PS[1 + (n % 2)]
                        specs = [(pz.ap[:, qs], kT_, q_, True, nm is None)]
                        rd = [sbk.b, sbq.b]
                        if nm is not None:
                            specs.append((pz.ap[:, qs], idb.ap, nm, False, True))
                            rd += [idb.b, nmask.b]
                        mm(specs, rd, [pz.b])
                        e_, sp_ = T32[1 + (n % 2)], T32[3 + (n % 4)]
                        act(e_.ap[:, qs], pz.ap[:, qs], AF.Exp, [pz.b], [e_.b])
                        act(sp_.ap[:, qs], e_.ap[:, qs], AF.Ln, [e_.b], [sp_.b], bias=1.0)

                    def k1(n):
                        h, bi, b, r, q0, qs, nm, hp, hc, kT_, q_ = geo(n)
                        sp_ = T32[3 + (n % 4)]
                        spb = T16[n % 3]
                        cp("dve", spb.ap[:, qs], sp_.ap[:, qs], [sp_.b], [spb.b])
                        if b > 0:
                            if bi == 0:
                                if q0 > 0:
                                    kb.op("pool", lambda h_, a=acc.ap: h_.memset(a, 0.0), writes=[acc.b])
                                cp("pool", acc.ap[:, qs], sp_.ap[:, qs], [sp_.b], [acc.b])
                            else:
                                tt("pool", acc.ap[:, qs], acc.ap[:, qs], sp_.ap[:, qs], ALU.add, [acc.b, sp_.b], [acc.b])

                    def k2(n):
                        h, bi, b, r, q0, qs, nm, hp, hc, kT_, q_ = geo(n)
                        spb = T16[n % 3]
                        accb = T16[3 + (n % 2)]
                        accn = T16[3 + ((n + 1) % 2)]
                        pr_ = PS[3 + (n % 2)]
                        specs = [(pr_.ap[:, qs], kT_, q_, True, False),
                                 (pr_.ap[:, qs], ntrib.ap, spb.ap[:, qs], False, bi == 0 and nm is None)]
                        rd = [sbk.b, sbq.b, ntrib.b, spb.b]
                        if bi > 0:
                            specs.append((pr_.ap[:, qs], noneb.ap, accb.ap[:, qs], False, nm is None))
                            rd += [accb.b, noneb.b]
                        if nm is not None:
                            specs.append((pr_.ap[:, qs], idb.ap, nm, False, True))
                            rd += [idb.b, nmask.b]
                        mm(specs, rd, [pr_.b])
                        if b > 0:
                            cp("dve", accn.ap, acc.ap, [acc.b], [accn.b])

                    def k3(n):
                        h, bi, b, r, q0, qs, nm, hp, hc, kT_, q_ = geo(n)
                        e_, sp_ = T32[1 + (n % 2)], T32[3 + (n % 4)]
                        pr_ = PS[3 + (n % 2)]
                        tt("dve", e_.ap[:, qs], pr_.ap[:, qs], sp_.ap[:, qs], ALU.subtract, [pr_.b, sp_.b], [e_.b])
                        wb = T16[5] if n % 2 == 0 else wb2
                        act(wb.ap[:, qs], e_.ap[:, qs], AF.Exp, [e_.b], [wb.b])

                    def k4(n):
                        h, bi, b, r, q0, qs, nm, hp, hc, kT_, q_ = geo(n)
                        wb = T16[5] if n % 2 == 0 else wb2
                        po = PS[0] if h % 2 == 0 else PS[6]
                        mm([(po.ap[:, qs], sbv.ap[:, b, hc * 128:(hc + 1) * 128], wb.ap[:, qs], bi == 0, b == 0)], [sbv.b, wb.b], [po.b], sgc=True)
                        if b == 0:
                            cp("act", oT.ap[hp, 4 + hc, :], po.ap[hp, :], [po.b], [oT.b])

                    stages_ = (k0, k1, k2, k3, k4)
                    for tck in range(NP_ + 4):
                        for kk_ in (4, 3, 2, 1, 0):
                            n = tck - kk_
                            if 0 <= n < NP_:
                                jstate["bank"], jstate["n"] = (PS[5], 1) if kk_ in (0, 2) else (None, 0)
                                stages_[kk_](n)
                    jstate["bank"] = None

                    kb.stage = "A7wout"
                    if c == 0:
                        for cg in range(2):
                            wout_i[cg], wout_t[cg] = ws_next((l, "out", cg))
                    for oc in range(8):
                        ps = PS[1 + (oc % 3)]
                        wt = wout_t[oc // 4]
                        mm([(ps.ap, wt.ap[:, k, (oc % 4) * 128:(oc % 4 + 1) * 128], oT.ap[:, k, :], k == 0, k == 7) for k in range(8)], [wt.b, oT.b], [ps.b])
                        tt("dve", xT.ap[:, oc, c0:c0 + 512], xT.ap[:, oc, c0:c0 + 512], ps.ap, ALU.add, [xT.b, ps.b], [xT.b])
                for cg in range(2):
                    ws_release(wout_i[cg])

                conv_upto(l, "f2")
                kb.stage = "Fnorm"
                kb.barrier()
                ar = Arena(SCR0, MEMB)
                pl = ar.a("plF", NPL, F32)
                drv = ar.a("drvF", 64, F32)
                h2 = ar.a("h2", 8 * S, BF16, (8, S))
                aT = ar.a("aT", 8 * 512, BF16, (8, 512))
                aT2 = ar.a("aT2", 8 * 512, BF16, (8, 512))
                T32 = [ar.a("t32F_%d" % i, 512, F32) for i in range(4)]
                kb.dma("sp", pl.ap, pl_d[l], pl.b, writes=[pl.b])
                ts("dve", drv.ap[:, 0:16], pl.ap[:, P_G1:P_G1 + 16], 32.0, None, ALU.mult, None, [pl.b], [drv.b])
                for c in range(NCH):
                    c0 = c * 512
                    act(aT.ap, xT.ap[:, :, c0:c0 + 512], AF.Square, [xT.b], [aT.b])
                    mm([(PS[0].ap, oneb.ap, aT.ap[:, k, :], k == 0, k == 7) for k in range(8)], [oneb.b, aT.b], [PS[0].b])
                    rsqrt_psum(T32[0], PS[0], 1024.0 * EPS, T32[1])
                    for k in range(8):
                        stt(h2.ap[:, k, c0:c0 + 512], xT.ap[:, k, c0:c0 + 512], drv.ap[:, 8 + k:9 + k], T32[0].ap,
                            ALU.mult, ALU.mult, [xT.b, drv.b, T32[0].b], [h2.b])
                kb.stage = "FFN"
                aTs = [aT, aT2]
                work = [(fg, c) for fg in range(4) for c in range(NCH)]
                held = {}

                def ff1(n, fg, c):
                    if c == 0:
                        conv_next(3)
                        if fg == 3 and l + 1 < L:
                            conv_upto(l + 1, "out")
                        held[fg] = [ws_next((l, "f1", 2 * fg)), ws_next((l, "f1", 2 * fg + 1))]
                    w1a, w1b = held[fg][0][1], held[fg][1][1]
                    c0 = c * 512
                    at = aTs[n % 2]
                    for f in range(8):
                        w1 = w1a if f < 4 else w1b
                        ps = PS[f % 3]
                        mm([(ps.ap, w1.ap[:, k, (f % 4) * 128:(f % 4 + 1) * 128], h2.ap[:, k, c0:c0 + 512], k == 0, k == 7) for k in range(8)],
                           [w1.b, h2.b], [ps.b])
                        rl = T32[2 + (f % 2)]
                        act(rl.ap, ps.ap, AF.Relu, [ps.b], [rl.b])
                        act(at.ap[:, f, :], rl.ap, AF.Square, [rl.b], [at.b])
                    if c == NCH - 1:
                        ws_release(held[fg][0][0])
                        ws_release(held[fg][1][0])

                def ff2(n, fg, c):
                    if c == 0:
                        held[fg] += [ws_next((l, "f2", fg, 0)), ws_next((l, "f2", fg, 1))]
                    w2a, w2b = held[fg][2][1], held[fg][3][1]
                    c0 = c * 512
                    at = aTs[n % 2]
                    for oc in range(8):
                        w2 = w2a if oc < 4 else w2b
                        ps = PS[3 + (oc % 3)]
                        mm([(ps.ap, w2.ap[:, k, (oc % 4) * 128:(oc % 4 + 1) * 128], at.ap[:, k, :], k == 0, k == 7) for k in range(8)],
                           [w2.b, at.b], [ps.b])
                        tt("dve", xT.ap[:, oc, c0:c0 + 512], xT.ap[:, oc, c0:c0 + 512], ps.ap, ALU.add, [xT.b, ps.b], [xT.b])
                    if c == NCH - 1:
                        for (i_, _t) in held.pop(fg)[2:]:
                            ws_release(i_)

                prevw = None
                for n, wk in enumerate(work + [None]):
                    if wk is not None:
                        ff1(n, *wk)
                    if prevw is not None:
                        ff2(n - 1, *prevw)
                    prevw = wk

            kb.stage = "store"
            kb.barrier()
            ar = Arena(SCR0, MEMB)
            yo = [ar.a("yo%d" % i, 1024, F32) for i in range(2)]
            for t in range(NT):
                yt = yo[t % 2]
                for hf in range(2):
                    ps = PS[(2 * t + hf) % 4]
                    tr([(ps.ap[:, j * 128:(j + 1) * 128], xT.ap[:, hf * 4 + j, t * 128:(t + 1) * 128], ident) for j in range(4)],
                       [xT.b, cst.b], [ps.b])
                    cp("act" if hf else "dve", yt.ap[:, hf * 512:(hf + 1) * 512], ps.ap, [ps.b], [yt.b])
                kb.dma("sp", y_d[seq, t * 128:(t + 1) * 128, :], yt.ap, yt.b, reads=[yt.b])

        with nc.Block() as block:
            kb.finish(block)
        print("prog sizes:", {n: len(e.prog) for n, e in kb.eng.items()}, "sems:", len(kb.sems))
        global LAST_LABELS
        LAST_LABELS = {n: list(e.labels) for n, e in kb.eng.items()}
    return nc


LAST_LABELS = None


_NC_CACHE = {}


def run(inputs, NSEQ, S, L, n_cores):
    key = (NSEQ, S, L)
    if key not in _NC_CACHE:
        _NC_CACHE[key] = build(NSEQ, S, L)
    nc = _NC_CACHE[key]
    f32 = lambda a: np.ascontiguousarray(np.asarray(a, dtype=np.float32))
    inp = {k: f32(v) for k, v in inputs.items()}
    pl = make_params(inp, L)
    cst = make_consts()
    x = inp["x"]
    in_maps = []
    for ci in range(n_cores):
        in_maps.append({
            "x": np.ascontiguousarray(x[ci * NSEQ:(ci + 1) * NSEQ]),
            "w_in": inp["w_in"], "w_out": inp["w_out"], "w_ff1": inp["w_ff1"], "w_ff2": inp["w_ff2"],
            "pl": pl, "cst": cst,
        })
    res = run_bass_kernel_spmd(nc, in_maps, core_ids=list(range(n_cores)))
    return np.concatenate([np.asarray(r["y"]) for r in res.results], axis=0)


def kernel(**inputs):
    x = np.asarray(inputs["x"])
    B, S, D = x.shape
    L = np.asarray(inputs["w_in"]).shape[0]
    NSEQ = B // N_CORES
    return run(inputs, NSEQ, S, L, N_CORES).astype(np.float32)
```

```python
import numpy as np
import concourse.bass as bass
import concourse.mybir as mybir
from concourse.bass_utils import run_bass_kernel_spmd
from contextlib import ExitStack

F32 = mybir.dt.float32
BF16 = mybir.dt.bfloat16
AF = mybir.ActivationFunctionType
ALU = mybir.AluOpType
AX = mybir.AxisListType

EPS = 1e-6
N_CORES = 8


class Buf:
    __slots__ = ("name", "w", "r", "excl", "dsem", "dcnt")

    def __init__(self, name, excl=False):
        self.name = name
        self.w = None
        self.r = []
        self.excl = excl
        self.dsem = None
        self.dcnt = 0


class Eng:
    def __init__(self, name, sem):
        self.name = name
        self.sem = sem
        self.cnt = 0
        self.known = {}
        self.prog = []
        self.labels = []
        self.hist = []


class KB:
    def __init__(self, nc, stack):
        self.nc = nc
        self.stack = stack
        self.sems = {}
        self.eng = {}
        self.dbufs = []
        self.stage = "pre"
        self.capture = None
        for n in ("pe", "act", "dve", "pool", "sp"):
            s = stack.enter_context(nc.semaphore("s_" + n))
            self.sems["e_" + n] = s
            self.eng[n] = Eng(n, "e_" + n)
        self.by_sem = {e.sem: e for e in self.eng.values()}

    def _deps(self, e, reads, writes):
        deps = {}

        def add(ev):
            if ev is None:
                return
            k, v, en = ev
            if en == "pe" and e.name == "pe":
                return
            if deps.get(k, 0) < v:
                deps[k] = v

        for b in reads:
            add(b.w)
            if b.excl:
                for ev in b.r:
                    if ev[2] != e.name:
                        add(ev)
        for b in writes:
            add(b.w)
            for ev in b.r:
                add(ev)
        waits = []
        snaps = []
        for k, v in sorted(deps.items(), key=lambda kv: -kv[1]):
            if e.known.get(k, 0) >= v:
                continue
            e.known[k] = v
            waits.append((k, v))
            src = self.by_sem.get(k)
            snap = src.hist[v - 1] if src is not None else None
            snaps.append(snap)
            if snap:
                for kk, vv in snap.items():
                    if e.known.get(kk, 0) < vv:
                        e.known[kk] = vv
        if len(waits) > 1:
            keep = []
            for i, (k, v) in enumerate(waits):
                implied = any(sn is not None and sn.get(k, 0) >= v for j, sn in enumerate(snaps) if j != i)
                if not implied:
                    keep.append((k, v))
            waits = keep
        return waits

    def op(self, en, fns, reads=(), writes=()):
        if self.capture is not None:
            self.capture.append((en, fns, tuple(reads), tuple(writes), self.stage))
            return
        e = self.eng[en]
        if callable(fns):
            fns = [fns]
        waits = self._deps(e, reads, writes)
        e.cnt += 1
        e.hist.append(dict(e.known))
        ev = (e.sem, e.cnt, e.name)
        for i, fn in enumerate(fns):
            last = i == len(fns) - 1
            e.prog.append((waits if i == 0 else [], fn, (e.sem, 1) if last else None))
            e.labels.append(self.stage)
        for b in reads:
            b.r.append(ev)
        for b in writes:
            b.w = ev
            b.r = []

    def cap_begin(self):
        self.capture = []

    def cap_end(self):
        c, self.capture = self.capture, None
        return c

    def emit(self, rec):
        en, fns, r, w, st = rec
        self.stage = st
        self.op(en, fns, r, w)

    def dma(self, en, out, in_, dbuf, reads=(), writes=()):
        e = self.eng[en]
        if dbuf.dsem is None:
            key = "d_%d" % len(self.sems)
            self.sems[key] = self.stack.enter_context(self.nc.semaphore(key))
            dbuf.dsem = key
            self.dbufs.append(dbuf)
        waits = self._deps(e, reads, writes)
        dbuf.dcnt += 16
        ev = (dbuf.dsem, dbuf.dcnt, "dma")
        e.prog.append((waits, (lambda h, o=out, i=in_: h.dma_start(out=o, in_=i)), (dbuf.dsem, 16)))
        e.labels.append("dma:" + self.stage)
        for b in reads:
            b.r.append(ev)
        for b in writes:
            b.w = ev
            b.r = []

    def barrier(self):
        evs = [(e.sem, e.cnt) for e in self.eng.values() if e.cnt > 0]
        evs += [(b.dsem, b.dcnt) for b in self.dbufs]
        for e in self.eng.values():
            waits = []
            for k, v in evs:
                if e.known.get(k, 0) < v:
                    e.known[k] = v
                    waits.append((k, v))
            if waits:
                e.prog.append((waits, None, None))

    def finish(self, block):
        self.barrier()
        sems = self.sems
        engs = self.eng

        def replay(h, name):
            for waits, fn, inc in engs[name].prog:
                for k, v in waits:
                    h.wait_ge(sems[k], v)
                if fn is None:
                    continue
                ins = fn(h)
                if inc is not None:
                    ins.then_inc(sems[inc[0]], inc[1])

        @block.tensor
        def _(h):
            replay(h, "pe")

        @block.scalar
        def _(h):
            replay(h, "act")

        @block.vector
        def _(h):
            replay(h, "dve")

        @block.gpsimd
        def _(h):
            replay(h, "pool")

        @block.sync
        def _(h):
            replay(h, "sp")


class T:
    __slots__ = ("ap", "b")

    def __init__(self, ap, b):
        self.ap = ap
        self.b = b


C_ID, C_TRI, C_BLK, C_MI, C_MS, C_MIT, C_MW, C_NTRI, C_NH, NCST = 0, 128, 256, 384, 512, 640, 768, 1664, 1792, 1800
P_G1, P_G2, P_CONV, P_ALOG, P_DTB, P_DNG, P_SBQ, P_SBK, P_SGB, P_SGVG, P_SGWT, NPL = 0, 8, 16, 64, 68, 72, 73, 74, 76, 80, 336, 848


def make_consts():
    c = np.zeros((128, NCST), np.float32)
    i = np.arange(128)
    c[:, C_ID:C_ID + 128] = np.eye(128)
    same = (i[:, None] // 64) == (i[None, :] // 64)
    c[:, C_TRI:C_TRI + 128] = same & (i[:, None] <= i[None, :])
    c[:, C_BLK:C_BLK + 128] = same
    c[:, C_MI:C_MI + 128] = same & (i[None, :] <= i[:, None])
    c[:, C_MS:C_MS + 128] = same & (i[None, :] < i[:, None])
    c[:, C_MIT:C_MIT + 128] = i[None, :] >= i[:, None]
    u = np.arange(896)
    c[:, C_MW:C_MW + 896] = (u[None, :] - 384) > i[:, None]
    c[:, C_NTRI:C_NTRI + 128] = -1.0 * (i[:, None] > i[None, :])
    c[:, C_NH] = -0.5
    return c


def make_params(inp, L):
    pl = np.zeros((L, 128, NPL), np.float32)
    for l in range(L):
        pl[l, :, P_G1:P_G1 + 8] = inp["norm1_g"][l].reshape(8, 128).T
        pl[l, :, P_G2:P_G2 + 8] = inp["norm2_g"][l].reshape(8, 128).T
        pl[l, :, P_CONV:P_CONV + 48] = inp["conv_w"][l].reshape(4, 12, 128).transpose(2, 1, 0).reshape(128, 48)
        pl[l, :, P_ALOG:P_ALOG + 4] = inp["a_log"][l][None, :]
        pl[l, :, P_DTB:P_DTB + 4] = inp["dt_bias"][l][None, :]
        pl[l, :, P_DNG] = inp["dn_out_g"][l]
        pl[l, :, P_SBQ] = np.tile(inp["sb_q_g"][l], 2)
        pl[l, :, P_SBK] = np.tile(inp["sb_k_g"][l], 2)
        pl[l, :, P_SGB:P_SGB + 4] = inp["sg_b"][l].T
        pl[l, :, P_SGVG:P_SGVG + 256] = inp["sg_v_g"][l][None, :]
        pl[l, :, P_SGWT:P_SGWT + 512] = inp["sg_w"][l].transpose(2, 0, 1).reshape(128, 512)
    return pl


WIN_SLICES = [
    [(0, 512)], [(512, 512)], [(1024, 512)], [(1536, 512)],
    [(2056, 512)],
    [(2568, 256), (2048, 8)],
    [(2824, 512)],
]


WIN_ORDER = [0, 1, 2, 3, 5, 6, 4]
JUNK = True


def build(NSEQ, S, L, dbg=False):
    NCH = S // 512
    NT = S // 128
    nc = bass.Bass("TRN2", target_bir_lowering=False)
    x_d = nc.dram_tensor("x", [NSEQ, S, 1024], F32, kind="ExternalInput").ap()
    win_d = nc.dram_tensor("w_in", [L, 1024, 3336], F32, kind="ExternalInput").ap()
    wout_d = nc.dram_tensor("w_out", [L, 1024, 1024], F32, kind="ExternalInput").ap()
    wf1_d = nc.dram_tensor("w_ff1", [L, 1024, 4096], F32, kind="ExternalInput").ap()
    wf2_d = nc.dram_tensor("w_ff2", [L, 4096, 1024], F32, kind="ExternalInput").ap()
    pl_d = nc.dram_tensor("pl", [L, 128, NPL], F32, kind="ExternalInput").ap()
    cst_d = nc.dram_tensor("cst", [128, NCST], F32, kind="ExternalInput").ap()
    y_d = nc.dram_tensor("y", [NSEQ, S, 1024], F32, kind="ExternalOutput").ap()
    dbg_d = nc.dram_tensor("dbg", [128, 8, 512], F32, kind="ExternalOutput").ap() if dbg else None

    st = ExitStack()
    with st:
        kb = KB(nc, st)
        MEMB = 212480
        M = st.enter_context(nc.sbuf_tensor("M", [128, MEMB // 4], F32))

        class Arena:
            def __init__(self, base, limit):
                self.off = base
                self.limit = limit

            def a(self, name, nelem, dtype, shape=None):
                nbytes = nelem * (4 if dtype == F32 else 2)
                nbytes = (nbytes + 63) // 64 * 64
                o = self.off
                self.off += nbytes
                assert self.off <= self.limit, (name, self.off, self.limit)
                ap = M[:, o // 4:(o + nbytes) // 4]
                if dtype == BF16:
                    ap = ap.bitcast(BF16)
                ap = ap[:, 0:nelem]
                if shape is not None:
                    if len(shape) == 2:
                        ap = ap.rearrange("p (a b) -> p a b", a=shape[0])
                    elif len(shape) == 3:
                        ap = ap.rearrange("p (a b c) -> p a b c", a=shape[0], b=shape[1])
                return T(ap, Buf(name))

        fixed = Arena(0, MEMB)
        xT = fixed.a("xT", 8 * S, F32, (8, S))
        NSLOT = 5
        slots = [fixed.a("slot%d" % i, 8 * 512, BF16, (8, 512)) for i in range(NSLOT)]
        cst = fixed.a("cst", NCST, F32)
        idb = fixed.a("idb", 128, BF16)
        oneb = fixed.a("oneb", 128, BF16)
        blkb = fixed.a("blkb", 128, BF16)
        ntrib = fixed.a("ntrib", 128, BF16)
        noneb = fixed.a("noneb", 128, BF16)
        nmask = fixed.a("nmask", 896, BF16)
        SCR0 = fixed.off

        PS = []
        for i in range(7):
            t = st.enter_context(nc.psum_tensor("ps%d" % i, [128, 512], F32))
            PS.append(T(t[:], Buf("ps%d" % i, excl=True)))
        tpb = st.enter_context(nc.psum_tensor("psb", [128, 1024], BF16))
        PB = T(tpb[:], Buf("psb", excl=True))

        def cs(off, n=128):
            return cst.ap[:, off:off + n]

        ident = cs(C_ID)
        def act(out, in_, func, r, w, **kw):
            kb.op("act", lambda h: h.activation(out=out, in_=in_, func=func, **kw), reads=r, writes=w)

        def tt(en, out, in0, in1, op, r, w):
            kb.op(en, lambda h: h.tensor_tensor(out=out, in0=in0, in1=in1, op=op), reads=r, writes=w)

        def ts(en, out, in0, s1, s2, op0, op1, r, w):
            if s2 is None:
                kb.op(en, lambda h: h.tensor_scalar(out=out, in0=in0, scalar1=s1, scalar2=None, op0=op0), reads=r, writes=w)
            else:
                kb.op(en, lambda h: h.tensor_scalar(out=out, in0=in0, scalar1=s1, scalar2=s2, op0=op0, op1=op1), reads=r, writes=w)

        def stt(out, in0, sc, in1, op0, op1, r, w):
            kb.op("dve", lambda h: h.scalar_tensor_tensor(out=out, in0=in0, scalar=sc, in1=in1, op0=op0, op1=op1), reads=r, writes=w)

        def cp(en, out, in_, r, w):
            if en == "act":
                act(out, in_, AF.Copy, r, w)
            else:
                kb.op(en, lambda h: h.tensor_copy(out=out, in_=in_), reads=r, writes=w)

        def mm(specs, r, w, sgc=False):
            if sgc:
                fns = [(lambda h, o=o, l=l, rr=rr, s=s, e=e: h.matmul(o, lhsT=l, rhs=rr, start=s, stop=e, skip_group_check=True)) for (o, l, rr, s, e) in specs]
            else:
                fns = [(lambda h, o=o, l=l, rr=rr, s=s, e=e: h.matmul(o, lhsT=l, rhs=rr, start=s, stop=e)) for (o, l, rr, s, e) in specs]
            kb.op("pe", fns, reads=r, writes=w)
            junk()

        def tr(specs, r, w):
            fns = [(lambda h, o=o, i=i, d=d: h.transpose(o, i, d)) for (o, i, d) in specs]
            kb.op("pe", fns, reads=r, writes=w)
            junk()

        jstate = {"bank": None, "n": 0}

        def junk():
            jb = jstate["bank"]
            if not JUNK or jb is None or jstate["n"] <= 0:
                return
            fns = [(lambda h, o=jb.ap: h.matmul(o, lhsT=idb.ap, rhs=nmask.ap[:, 0:512], start=True, stop=True)) for _ in range(jstate["n"])]
            kb.op("pe", fns, reads=[idb.b, nmask.b], writes=[jb.b])

        def rsqrt_psum(out_t, ps_t, addc, tmp_t, cols=512, scale=1.0):
            act(tmp_t.ap[:, 0:cols], ps_t.ap[:, 0:cols], AF.Ln, [ps_t.b], [tmp_t.b], bias=addc, scale=scale)
            act(out_t.ap[:, 0:cols], tmp_t.ap[:, 0:cols], AF.Exp, [tmp_t.b], [out_t.b], scale=-0.5)

        kb.dma("sp", cst.ap, cst_d, cst.b, writes=[cst.b])
        cp("dve", idb.ap, ident, [cst.b], [idb.b])
        cp("dve", blkb.ap, cs(C_BLK), [cst.b], [blkb.b])
        cp("dve", ntrib.ap, cs(C_NTRI), [cst.b], [ntrib.b])
        kb.op("pool", lambda h: h.memset(oneb.ap, 1.0), writes=[oneb.b])
        kb.op("pool", lambda h: h.memset(noneb.ap, -1.0), writes=[noneb.b])
        ts("dve", nmask.ap, cst.ap[:, C_MW:C_MW + 896], -1.0, 30000.0, ALU.add, ALU.mult, [cst.b], [nmask.b])

        wsc = {}

        def conv_block(key, ncols, pieces):
            th = nc.dram_tensor("wsc_%s" % "_".join(str(k) for k in key), [1024, ncols], BF16)
            ap = th.ap()
            b = Buf("wsc")
            c0 = 0
            for (src, n) in pieces:
                kb.dma("pool", ap[:, c0:c0 + n], src, b, writes=[b])
                c0 += n
            wsc[key] = (ap.rearrange("(k p) n -> p k n", p=128), b, ncols)

        conv_pending = []
        for l in range(L):
            for si, pieces in enumerate(WIN_SLICES):
                ncols = sum(n for _, n in pieces)
                conv_pending.append(((l, "in", si), ncols, [(win_d[l, :, c0:c0 + n], n) for (c0, n) in pieces]))
            for cg in range(2):
                conv_pending.append(((l, "out", cg), 512, [(wout_d[l, :, cg * 512:(cg + 1) * 512], 512)]))
            for cg in range(8):
                conv_pending.append(((l, "f1", cg), 512, [(wf1_d[l, :, cg * 512:(cg + 1) * 512], 512)]))
            for rg in range(4):
                for cg in range(2):
                    conv_pending.append(((l, "f2", rg, cg), 512, [(wf2_d[l, rg * 1024:(rg + 1) * 1024, cg * 512:(cg + 1) * 512], 512)]))

        def conv_next(n=1):
            for _ in range(n):
                if conv_pending:
                    conv_block(*conv_pending.pop(0))

        def conv_upto(l, name):
            idx = [i for i, (k, _, _) in enumerate(conv_pending) if k[0] == l and k[1] == name]
            if idx:
                conv_next(idx[-1] + 1)

        conv_upto(0, "out")

        sched = []
        for s in range(NSEQ):
            for l in range(L):
                for c in range(NCH):
                    for si in WIN_ORDER:
                        sched.append((l, "in", si))
                    if c == 0:
                        sched.append((l, "out", 0))
                        sched.append((l, "out", 1))
                for fg in range(4):
                    sched += [(l, "f1", 2 * fg), (l, "f1", 2 * fg + 1), (l, "f2", fg, 0), (l, "f2", fg, 1)]
        ws_state = {"pos": 0, "use": 0, "free": list(range(NSLOT)), "loaded": {}}

        def ws_fill():
            while ws_state["free"] and ws_state["pos"] < len(sched):
                sl = ws_state["free"].pop(0)
                key = sched[ws_state["pos"]]
                src, sb, ncols = wsc[key]
                kb.dma("sp", slots[sl].ap[:, :, 0:ncols], src, slots[sl].b, reads=[sb], writes=[slots[sl].b])
                ws_state["loaded"][ws_state["pos"]] = sl
                ws_state["pos"] += 1

        def ws_next(key):
            i = ws_state["use"]
            assert sched[i] == key, (sched[i], key)
            ws_state["use"] += 1
            assert i in ws_state["loaded"], "slice not prefetched (deadlock): %s" % (key,)
            return i, slots[ws_state["loaded"][i]]

        def ws_release(i):
            ws_state["free"].append(ws_state["loaded"].pop(i))
            ws_fill()

        for seq in range(NSEQ):
            kb.stage = "xload"
            kb.barrier()
            ar = Arena(SCR0, MEMB)
            xin = [ar.a("xin%d" % i, 1024, F32) for i in range(2)]
            for t in range(NT):
                xi = xin[t % 2]
                kb.dma("sp", xi.ap, x_d[seq, t * 128:(t + 1) * 128, :], xi.b, writes=[xi.b])
                for hf in range(2):
                    ps = PS[(2 * t + hf) % 4]
                    tr([(ps.ap[:, j * 128:(j + 1) * 128], xi.ap[:, (hf * 4 + j) * 128:(hf * 4 + j + 1) * 128], ident) for j in range(4)],
                       [xi.b, cst.b], [ps.b])
                    cp("act" if hf else "dve", xT.ap[:, hf * 4:hf * 4 + 4, t * 128:(t + 1) * 128],
                       ps.ap.rearrange("p (a b) -> p a b", a=4), [ps.b], [xT.b])

            if seq == 0:
                ws_fill()
            for l in range(L):
                kb.stage = "Asetup"
                kb.barrier()
                ar = Arena(SCR0, MEMB)
                pl = ar.a("pl", NPL, F32)
                drv = ar.a("drv", 64, F32)
                sgwm = ar.a("sgwm", 512, BF16, (4, 128))
                diag = [ar.a("diag%d" % i, 512, BF16, (4, 128)) for i in range(2)]
                hT = ar.a("hT", 8 * 512, BF16, (8, 512))
                oT = hT
                dq = ar.a("dq", 4 * 512, BF16, (4, 512))
                dk = ar.a("dk", 4 * 512, BF16, (4, 512))
                dv = ar.a("dv", 4 * 512, BF16, (4, 512))
                dz = ar.a("dz", 4 * 512, BF16, (4, 512))
                sbq = ar.a("sbq", 2 * 512, BF16, (2, 512))
                sbk = ar.a("sbk", 2 * S, BF16, (2, S))
                sbv = ar.a("sbv", NT * 256, BF16, (NT, 256))
                sgu = ar.a("sgu", 4 * 256, BF16, (4, 256))
                sgv = ar.a("sgv", 4 * 256, BF16, (4, 256))
                oraw = ar.a("oraw", 4 * 512, BF16, (4, 512))
                pre = [ar.a("pre%d" % i, 520, BF16) for i in range(2)]
                halo = ar.a("halo", 12 * 4, BF16, (12, 4))
                S32 = ar.a("S32", 512, F32, (4, 128))
                Sbfs = [ar.a("Sbf%d" % i, 512, BF16, (4, 128)) for i in range(2)]
                sm = ar.a("sm", 16 * 16, F32, (16, 16))
                gg = ar.a("gg", 32, F32, (4, 8))
                glb = ar.a("glb", 32, F32)
                T32 = [ar.a("t32_%d" % i, 512, F32) for i in range(8)]
                T16 = [ar.a("t16_%d" % i, 512, BF16) for i in range(6)]
                _su = sgu.ap.rearrange("p a b -> p (a b)")
                _sv = sgv.ap.rearrange("p a b -> p (a b)")
                T16b = [T(_su[:, 0:512], sgu.b), T(_su[:, 512:1024], sgu.b), T(_sv[:, 0:512], sgv.b), T(_sv[:, 512:1024], sgv.b)]
                Upar = [ar.a("Upar%d" % i, 512, F32) for i in range(2)]
                SdT = ar.a("SdT", 512, F32)
                glb2 = ar.a("glb2", 32, F32)

                kb.dma("sp", pl.ap, pl_d[l], pl.b, writes=[pl.b])
                ts("dve", drv.ap[:, 0:16], pl.ap[:, P_G1:P_G1 + 16], 32.0, None, ALU.mult, None, [pl.b], [drv.b])
                act(drv.ap[:, 16:20], pl.ap[:, P_ALOG:P_ALOG + 4], AF.Exp, [pl.b], [drv.b])
                ts("dve", drv.ap[:, 16:20], drv.ap[:, 16:20], -1.0, None, ALU.mult, None, [drv.b], [drv.b])
                ts("dve", drv.ap[:, 20:21], pl.ap[:, P_DNG:P_DNG + 1], 0.5, None, ALU.mult, None, [pl.b], [drv.b])
                ts("dve", drv.ap[:, 21:22], pl.ap[:, P_SBQ:P_SBQ + 1], 1.0, None, ALU.mult, None, [pl.b], [drv.b])
                ts("dve", drv.ap[:, 22:23], pl.ap[:, P_SBK:P_SBK + 1], 8.0, None, ALU.mult, None, [pl.b], [drv.b])
                ts("dve", drv.ap[:, 24:28], pl.ap[:, P_SGB:P_SGB + 4], 0.5, None, ALU.mult, None, [pl.b], [drv.b])
                stt(sgwm.ap, pl.ap[:, P_SGWT:P_SGWT + 512].rearrange("p (g t) -> p g t", g=4), 0.5,
                    cs(C_MIT).unsqueeze(1).to_broadcast([128, 4, 128]), ALU.mult, ALU.mult, [pl.b, cst.b], [sgwm.b])
                kb.op("pool", lambda h: h.memset(S32.ap, 0.0), writes=[S32.b])
                kb.op("pool", lambda h: h.memset(Sbfs[0].ap, 0.0), writes=[Sbfs[0].b])
                kb.op("pool", lambda h: h.memset(halo.ap, 0.0), writes=[halo.b])

                wout_i = [None, None]
                wout_t = [None, None]

                for c in range(NCH):
                    c0 = c * 512
                    conv_next(4 if NCH >= 4 else 16)
                    if c == NCH - 1:
                        conv_upto(l, "f2")
                    kb.stage = "A1norm"
                    def rmsnorm_chunk(gcol, dst, c0=c0):
                        act(dst.ap, xT.ap[:, :, c0:c0 + 512], AF.Square, [xT.b], [dst.b])
                        mm([(PS[0].ap, oneb.ap, dst.ap[:, k, :], k == 0, k == 7) for k in range(8)], [oneb.b, dst.b], [PS[0].b])
                        rsqrt_psum(T32[0], PS[0], 1024.0 * EPS, T32[1])
                        for k in range(8):
                            stt(dst.ap[:, k, :], xT.ap[:, k, c0:c0 + 512], drv.ap[:, gcol + k:gcol + k + 1], T32[0].ap,
                                ALU.mult, ALU.mult, [xT.b, drv.b, T32[0].b], [dst.b])
                    rmsnorm_chunk(0, hT)

                    kb.stage = "A2dnproj"
                    def proj_fm(slot, j, ps):
                        mm([(ps.ap, slot.ap[:, k, j * 128:(j + 1) * 128], hT.ap[:, k, :], k == 0, k == 7) for k in range(8)],
                           [slot.b, hT.b], [ps.b])

                    items = [(grp, hh) for grp in range(4) for hh in range(4)]
                    dsts = (dq, dk, dv, dz)
                    cur_w = {}

                    def emit_proj(grp, hh):
                        if hh == 0:
                            cur_w["w"] = ws_next((l, "in", grp))
                        wi, wsl = cur_w["w"]
                        j = grp * 4 + hh
                        ps = PS[1 + (j % 2)]
                        proj_fm(wsl, hh, ps)
                        if hh == 3:
                            ws_release(wi)
                        if grp < 3:
                            pr = pre[j % 2]
                            dg = diag[j % 2]
                            cp("pool", pr.ap[:, 0:3], halo.ap[:, j, 0:3], [halo.b], [pr.b])
                            cp("act", pr.ap[:, 3:515], ps.ap, [ps.b], [pr.b])
                            cp("pool", halo.ap[:, j, 0:3], pr.ap[:, 512:515], [pr.b], [halo.b])
                            for i in range(4):
                                act(dg.ap[:, i, :], idb.ap, AF.Copy, [idb.b, pl.b], [dg.b], scale=pl.ap[:, P_CONV + j * 4 + i:P_CONV + j * 4 + i + 1])

                    def emit_conv(grp, hh):
                        j = grp * 4 + hh
                        dst = dsts[grp]
                        if grp < 3:
                            pr = pre[j % 2]
                            dg = diag[j % 2]
                            pc = PS[3 + (j % 2)]
                            mm([(pc.ap, dg.ap[:, i, :], pr.ap[:, i:i + 512], i == 0, i == 3) for i in range(4)], [dg.b, pr.b], [pc.b])
                        else:
                            pc = PS[1 + (j % 2)]
                        th = T32[2 + (j % 2)]
                        act(th.ap, pc.ap, AF.Tanh, [pc.b], [th.b], scale=0.5)
                        stt(dst.ap[:, hh, :], th.ap, 1.0, pc.ap, ALU.add, ALU.mult, [th.b, pc.b], [dst.b])

                    prev = None
                    for it in items + [None]:
                        if it is not None:
                            emit_proj(*it)
                        if prev is not None:
                            emit_conv(*prev)
                        prev = it
                    kb.stage = "A3tok"
                    wi5, w5 = ws_next((l, "in", 5))
                    wi6, w6 = ws_next((l, "in", 6))
                    ab = sm.ap[:, 0:2, :].rearrange("p a (t e) -> p (a t) e", e=8)
                    def a3x(t4):
                        tg = c * 4 + t4
                        tc = slice(t4 * 128, (t4 + 1) * 128)
                        pa = PS[1 + (t4 % 2)]
                        mm([(pa.ap[:, 0:264], hT.ap[:, k, tc], w5.ap[:, k, 0:264], k == 0, k == 7) for k in range(8)], [hT.b, w5.b], [pa.b])
                        cp("act", sbv.ap[:, tg, :], pa.ap[:, 0:256], [pa.b], [sbv.b])
                        cp("dve", ab[:, t4, :], pa.ap[:, 256:264], [pa.b], [sm.b])
                        pg = PS[3 + (t4 % 2)]
                        mm([(pg.ap, hT.ap[:, k, tc], w6.ap[:, k, :], k == 0, k == 7) for k in range(8)], [hT.b, w6.b], [pg.b])
                        xs, t1, t2 = T32[0 + 3 * (t4 % 2)], T32[1 + 3 * (t4 % 2)], T32[2 + 3 * (t4 % 2)]
                        cp("act", xs.ap, pg.ap, [pg.b], [xs.b])
                        act(t1.ap, pg.ap, AF.Square, [pg.b], [t1.b])

                    def a3y(t4):
                        xs, t1, t2 = T32[0 + 3 * (t4 % 2)], T32[1 + 3 * (t4 % 2)], T32[2 + 3 * (t4 % 2)]
                        ts("dve", t1.ap, t1.ap, 0.044715, 1.0, ALU.mult, ALU.add, [t1.b], [t1.b])
                        tt("dve", t1.ap, t1.ap, xs.ap, ALU.mult, [t1.b, xs.b], [t1.b])
                        act(t2.ap, t1.ap, AF.Tanh, [t1.b], [t2.b], scale=0.7978845608028654)
                        stt(sgu.ap[:, t4, :], t2.ap[:, 0:256], 1.0, xs.ap[:, 0:256], ALU.add, ALU.mult, [t2.b, xs.b], [sgu.b])
                        stt(t1.ap[:, 256:512], t2.ap[:, 256:512], 1.0, xs.ap[:, 256:512], ALU.add, ALU.mult, [t2.b, xs.b], [t1.b])
                        act(t2.ap[:, 256:512], t1.ap[:, 256:512], AF.Square, [t1.b], [t2.b])
                        ssr = sm.ap[:, 2 + (t4 % 2), 0:4]
                        kb.op("dve", lambda h, o=ssr, i=t2.ap[:, 256:512].rearrange("p (g d) -> p g d", g=4): h.tensor_reduce(out=o, in_=i, axis=AX.X, op=ALU.add),
                              reads=[t2.b], writes=[sm.b])
                        ts("dve", ssr, ssr, 1.0 / 64.0, 4.0 * EPS, ALU.mult, ALU.add, [sm.b], [sm.b])
                        tt("pool", ssr, ssr, cst.ap[:, C_NH:C_NH + 1].to_broadcast([128, 4]), ALU.pow, [sm.b, cst.b], [sm.b])
                        tt("dve", t1.ap[:, 256:512].rearrange("p (g d) -> p g d", g=4), t1.ap[:, 256:512].rearrange("p (g d) -> p g d", g=4),
                           ssr.unsqueeze(2).to_broadcast([128, 4, 64]), ALU.mult, [t1.b, sm.b], [t1.b])
                        tt("pool", sgv.ap[:, t4, :], t1.ap[:, 256:512], pl.ap[:, P_SGVG:P_SGVG + 256], ALU.mult, [t1.b, pl.b], [sgv.b])

                    for t4 in range(5):
                        if t4 < 4:
                            a3x(t4)
                        if t4 >= 1:
                            a3y(t4 - 1)
                    ws_release(wi5)
                    ws_release(wi6)

                    kb.stage = "l2n_sbproj"
                    l2items = [("dn", grp, hh) for grp in range(2) for hh in range(4)] + [("sb", j, 0) for j in range(4)]
                    sbw = {}

                    PNB = (PS[5], PS[6], PS[3], PS[4])

                    def stX(n, kind, a_, b_):
                        sqb = T16[n % 4]
                        pn = PNB[n % 4]
                        if kind == "dn":
                            dst = (dq, dk)[a_]
                            tt("dve", sqb.ap, dst.ap[:, b_, :], dst.ap[:, b_, :], ALU.mult, [dst.b], [sqb.b])
                            mm([(pn.ap, oneb.ap, sqb.ap, True, True)], [oneb.b, sqb.b], [pn.b])
                        else:
                            j = a_
                            if j == 0:
                                sbw["w"] = ws_next((l, "in", 4))
                            wi, wsl = sbw["w"]
                            ps = PS[1 + (j % 2)]
                            proj_fm(wsl, j, ps)
                            if j == 3:
                                ws_release(wi)
                            xs = T32[(j % 4)]
                            cp("act", xs.ap, ps.ap, [ps.b], [xs.b])
                            tt("dve", sqb.ap, xs.ap, xs.ap, ALU.mult, [xs.b], [sqb.b])
                            mm([(pn.ap, blkb.ap, sqb.ap, True, True)], [blkb.b, sqb.b], [pn.b])

                    def stY(n, kind, a_, b_):
                        pn = PNB[n % 4]
                        rs = T32[4 + (n % 4)]
                        if kind == "dn":
                            dst = (dq, dk)[a_]
                            rsqrt_psum(rs, pn, 4.0 * EPS, rs)
                            if a_ == 0:
                                stt(dst.ap[:, b_, :], dst.ap[:, b_, :], 128.0 ** -0.5, rs.ap, ALU.mult, ALU.mult, [dst.b, rs.b], [dst.b])
                            else:
                                tt("dve", dst.ap[:, b_, :], dst.ap[:, b_, :], rs.ap, ALU.mult, [dst.b, rs.b], [dst.b])
                        else:
                            j = a_
                            xs = T32[(j % 4)]
                            rsqrt_psum(rs, pn, 64.0 * EPS, rs)
                            if j < 2:
                                stt(sbq.ap[:, j, :], xs.ap, drv.ap[:, 21:22], rs.ap, ALU.mult, ALU.mult, [xs.b, drv.b, rs.b], [sbq.b])
                            else:
                                stt(sbk.ap[:, j - 2, c0:c0 + 512], xs.ap, drv.ap[:, 22:23], rs.ap, ALU.mult, ALU.mult, [xs.b, drv.b, rs.b], [sbk.b])

                    for n in range(len(l2items) + 2):
                        if n < len(l2items):
                            stX(n, *l2items[n])
                        if n >= 2:
                            stY(n - 2, *l2items[n - 2])

                    kb.stage = "SGmix"
                    for t4 in range(4):
                        pm = PS[1 + (t4 % 2)]
                        for g in range(4):
                            mm([(pm.ap[:, g * 64:(g + 1) * 64], sgwm.ap[:, g, :], sgv.ap[:, t4, g * 64:(g + 1) * 64], True, True)],
                               [sgwm.b, sgv.b], [pm.b])
                        og = T16[2 + (t4 % 2)]
                        for g in range(4):
                            stt(og.ap[:, g * 64:(g + 1) * 64], pm.ap[:, g * 64:(g + 1) * 64], drv.ap[:, 24 + g:25 + g], sgu.ap[:, t4, g * 64:(g + 1) * 64],
                                ALU.add, ALU.mult, [pm.b, drv.b, sgu.b], [og.b])
                        tr([(PB.ap[:, jj * 128:(jj + 1) * 128], og.ap[:, jj * 128:(jj + 1) * 128], idb.ap) for jj in range(2)], [og.b, idb.b], [PB.b])
                        cp("act", oT.ap[:, 6:8, t4 * 128:(t4 + 1) * 128], PB.ap[:, 0:256].rearrange("p (a b) -> p a b", a=2), [PB.b], [oT.b])

                    kb.stage = "DNprep"
                    a_in = ab[:, :, 0:4]
                    b_in = ab[:, :, 4:8]
                    def smr(i):
                        return sm.ap[:, i, :].rearrange("p (t e) -> p t e", e=4)
                    g_t, beta, nbeta, vb, eg, el, bq, tmpa, tmpb = [smr(i) for i in range(4, 13)]
                    dtb_b = pl.ap[:, P_DTB:P_DTB + 4].unsqueeze(1).to_broadcast([128, 4, 4])
                    nA_b = drv.ap[:, 16:20].unsqueeze(1).to_broadcast([128, 4, 4])
                    tt("dve", tmpa, a_in, dtb_b, ALU.add, [sm.b, pl.b], [sm.b])
                    act(tmpa, tmpa, AF.Exp, [sm.b], [sm.b])
                    act(tmpa, tmpa, AF.Ln, [sm.b], [sm.b], bias=1.0)
                    tt("dve", g_t, tmpa, nA_b, ALU.mult, [sm.b, drv.b], [sm.b])
                    act(tmpb, b_in, AF.Exp, [sm.b], [sm.b], scale=-1.0)
                    ts("dve", tmpb, tmpb, 1.0, None, ALU.add, None, [sm.b], [sm.b])
                    kb.op("dve", lambda h, o=beta, i=tmpb: h.reciprocal(out=o, in_=i), reads=[sm.b], writes=[sm.b])
                    ts("dve", nbeta, beta, -1.0, None, ALU.mult, None, [sm.b], [sm.b])
                    ts("dve", vb, beta, 0.5, None, ALU.mult, None, [sm.b], [sm.b])
                    pgc = PS[0]
                    specs = []
                    for t4 in range(4):
                        specs.append((pgc.ap[:, t4 * 8:t4 * 8 + 4], cs(C_TRI), g_t[:, t4, :], True, True))
                        specs.append((pgc.ap[:, t4 * 8 + 4:t4 * 8 + 8], cs(C_BLK), g_t[:, t4, :], True, True))
                    mm(specs, [cst.b, sm.b], [pgc.b])
                    cp("dve", gg.ap, pgc.ap[:, 0:32].rearrange("p (t e) -> p t e", e=8), [pgc.b], [gg.b])
                    gc = gg.ap[:, :, 0:4]
                    gl = gg.ap[:, :, 4:8]
                    act(eg, gc, AF.Exp, [gg.b], [sm.b])
                    tt("dve", tmpa, gl, gc, ALU.subtract, [gg.b], [sm.b])
                    act(el, tmpa, AF.Exp, [sm.b], [sm.b])
                    tt("dve", bq, beta, eg, ALU.mult, [sm.b], [sm.b])

                    def dn_prep(t4):
                        kb.stage = "DNtile"
                        par = t4 % 2
                        glbT = glb if par == 0 else glb2
                        tc = slice(t4 * 128, (t4 + 1) * 128)
                        Gb, A, B, C_, D_, E_, F_, X_ = T32

                        def v4(t):
                            return t.ap.rearrange("p (h n) -> p h n", h=4)

                        def bc4(ap):
                            return ap.unsqueeze(2).to_broadcast([128, 4, 128])

                        def cb4(off):
                            return cs(off).unsqueeze(1).to_broadcast([128, 4, 128])

                        def hsl(h):
                            return slice(h * 128, (h + 1) * 128)

                        cp("act", v4(Gb), bc4(g_t[:, t4, :]), [sm.b], [Gb.b])
                        mm([(PS[4].ap[:, hsl(h)], v4(Gb)[:, h, :], cs(C_TRI), True, True) for h in range(4)], [Gb.b, cst.b], [PS[4].b])
                        mm([(PS[2].ap[:, h * 2:h * 2 + 2], v4(Gb)[:, h, :], cst.ap[:, C_BLK:C_BLK + 128:64], True, True) for h in range(4)],
                           [Gb.b, cst.b], [PS[2].b])
                        act(glbT.ap[:, 0:8], PS[2].ap[:, 0:8], AF.Exp, [PS[2].b], [glbT.b])
                        for h in range(4):
                            ts("dve", A.ap[:, hsl(h)], PS[4].ap[:, hsl(h)], gc[:, t4, h:h + 1], 0.0, ALU.subtract, ALU.max, [PS[4].b, gg.b], [A.b])
                        act(B.ap, A.ap, AF.Exp, [A.b], [B.b], scale=-1.0)
                        tt("pool", v4(C_), v4(B), cb4(C_MI), ALU.mult, [B.b, cst.b], [C_.b])
                        tt("pool", v4(D_), v4(B), cb4(C_MS), ALU.mult, [B.b, cst.b], [D_.b])
                        act(A.ap, PS[4].ap, AF.Exp, [PS[4].b], [A.b])
                        qd = T16[0] if par == 0 else T16b[0]
                        tt("pool", v4(qd), dq.ap[:, :, tc], v4(A), ALU.mult, [dq.b, A.b], [qd.b])
                        mm([(PS[2].ap[:, hsl(h)], dk.ap[:, h, tc], dk.ap[:, h, tc], True, True) for h in range(4)], [dk.b], [PS[2].b])
                        mm([(PS[3].ap[:, hsl(h)], dq.ap[:, h, tc], dk.ap[:, h, tc], True, True) for h in range(4)], [dq.b, dk.b], [PS[3].b])
                        for h in range(4):
                            stt(E_.ap[:, hsl(h)], PS[2].ap[:, hsl(h)], nbeta[:, t4, h:h + 1], D_.ap[:, hsl(h)], ALU.mult, ALU.mult,
                                [PS[2].b, sm.b, D_.b], [E_.b])
                        qkm = T16[1]
                        tt("dve", qkm.ap, PS[3].ap, C_.ap, ALU.mult, [PS[3].b, C_.b], [qkm.b])
                        tr([(PS[4].ap[:, hsl(h)], v4(E_)[:, h, :], ident) for h in range(4)], [E_.b, cst.b], [PS[4].b])
                        cp("act", F_.ap, PS[4].ap, [PS[4].b], [F_.b])
                        Xs = X_.ap[:, 0:256].rearrange("p (h n) -> p h n", h=4)
                        for hb in range(2):
                            hs_ = slice(hb * 64, hb * 64 + 64)
                            tt("dve", Xs[hs_], PS[4].ap.rearrange("p (h n) -> p h n", h=4)[hs_, :, hb * 64:hb * 64 + 64],
                               ident[hs_, hb * 64:hb * 64 + 64].unsqueeze(1).to_broadcast([64, 4, 64]), ALU.add, [PS[4].b, cst.b], [X_.b])
                        Qc, Pc = E_, F_
                        Qn, Pn = C_, D_
                        for lev in range(1, 6):
                            mm([(PS[2].ap[:, hsl(h)], v4(Pc)[:, h, :], v4(Qc)[:, h, :], True, True) for h in range(4)], [Pc.b, Qc.b], [PS[2].b])
                            if lev < 5:
                                mm([(PS[3].ap[:, hsl(h)], v4(Qc)[:, h, :], v4(Pc)[:, h, :], True, True) for h in range(4)], [Pc.b, Qc.b], [PS[3].b])
                            cp("act", Qn.ap, PS[2].ap, [PS[2].b], [Qn.b])
                            if lev < 5:
                                cp("dve", Pn.ap, PS[3].ap, [PS[3].b], [Pn.b])
                            mm([(PS[4].ap[:, h * 64:(h + 1) * 64], v4(Qn)[:, h, :], Xs[:, h, :], True, True) for h in range(4)], [Qn.b, X_.b], [PS[4].b])
                            tt("dve", X_.ap[:, 0:256], X_.ap[:, 0:256], PS[4].ap[:, 0:256], ALU.add, [X_.b, PS[4].b], [X_.b])
                            Qc, Qn = Qn, Qc
                            Pc, Pn = Pn, Pc
                        Xbd = Pn
                        for hb in range(2):
                            act(v4(Xbd)[:, :, hb * 64:hb * 64 + 64], Xs, AF.Copy, [X_.b, cst.b], [Xbd.b], scale=cst.ap[:, C_BLK + hb * 64:C_BLK + hb * 64 + 1])
                        X_ = Xbd
                        tr([(PB.ap[:, hsl(h)], dk.ap[:, h, tc], idb.ap) for h in range(4)], [dk.b, idb.b], [PB.b])
                        Kbe = A
                        tt("dve", v4(Kbe), PB.ap[:, 0:512].rearrange("p (h n) -> p h n", h=4), bc4(bq[:, t4, :]), ALU.mult, [PB.b, sm.b], [Kbe.b])
                        kdec = T16[3] if par == 0 else T16b[1]
                        tt("dve", v4(kdec), PB.ap[:, 0:512].rearrange("p (h n) -> p h n", h=4), bc4(el[:, t4, :]), ALU.mult, [PB.b, sm.b], [kdec.b])
                        tr([(PB.ap[:, 512 + h * 128:512 + (h + 1) * 128], dv.ap[:, h, tc], idb.ap) for h in range(4)], [dv.b, idb.b], [PB.b])
                        Vb = B
                        tt("dve", v4(Vb), PB.ap[:, 512:1024].rearrange("p (h n) -> p h n", h=4), bc4(vb[:, t4, :]), ALU.mult, [PB.b, sm.b], [Vb.b])
                        tr([(PB.ap[:, hsl(h)], v4(qkm)[:, h, :], idb.ap) for h in range(4)], [qkm.b, idb.b], [PB.b])
                        qkT = T16[4] if par == 0 else T16b[2]
                        cp("act", qkT.ap, PB.ap[:, 0:512], [PB.b], [qkT.b])
                        mm([(PS[2].ap[:, hsl(h)], v4(Kbe)[:, h, :], v4(X_)[:, h, :], True, True) for h in range(4)], [Kbe.b, X_.b], [PS[2].b])
                        WT = T16[2] if par == 0 else T16b[3]
                        cp("act", WT.ap, PS[2].ap, [PS[2].b], [WT.b])
                        mm([(PS[3].ap[:, hsl(h)], v4(X_)[:, h, :], v4(Vb)[:, h, :], True, True) for h in range(4)], [X_.b, Vb.b], [PS[3].b])
                        U = Upar[par]
                        cp("act", U.ap, PS[3].ap, [PS[3].b], [U.b])
                        return dict(qd=qd, kdec=kdec, qkT=qkT, WT=WT, U=U, glbT=glbT)

                    def dn_scan(t4, tb):
                        qd, kdec, qkT, WT, U, glbT = tb['qd'], tb['kdec'], tb['qkT'], tb['WT'], tb['U'], tb['glbT']

                        def v4(t):
                            return t.ap.rearrange("p (h n) -> p h n", h=4)

                        def hsl(h):
                            return slice(h * 128, (h + 1) * 128)

                        for half in range(2):
                            kb.stage = "DNscan"
                            hs = slice(half * 64, half * 64 + 64)
                            tcs = slice(t4 * 128 + half * 64, t4 * 128 + half * 64 + 64)
                            Sbf, SbfN = Sbfs[half], Sbfs[1 - half]
                            mm([(PS[5].ap[:, hsl(h)], v4(WT)[:, h, :], Sbf.ap[:, h, :], True, True) for h in range(4)], [WT.b, Sbf.b], [PS[5].b])
                            un = T16[5]
                            tt("dve", un.ap[hs, :], U.ap[hs, :], PS[5].ap[hs, :], ALU.subtract, [U.b, PS[5].b], [un.b])
                            mm([(PS[0].ap[:, hsl(h)], v4(kdec)[hs, h, :], v4(un)[hs, h, :], True, True) for h in range(4)], [kdec.b, un.b], [PS[0].b])
                            specs = []
                            for h in range(4):
                                specs.append((PS[6].ap[:, h * 64:(h + 1) * 64], Sbf.ap[:, h, :], v4(qd)[:, h, hs], True, False))
                                specs.append((PS[6].ap[:, h * 64:(h + 1) * 64], v4(un)[hs, h, :], v4(qkT)[hs, h, hs], False, True))
                            mm(specs, [Sbf.b, qd.b, un.b, qkT.b], [PS[6].b])
                            cp("act", oraw.ap[:, :, tcs], PS[6].ap[:, 0:256].rearrange("p (h n) -> p h n", h=4), [PS[6].b], [oraw.b])
                            Sd = SdT
                            tt("pool", v4(Sd), S32.ap, glbT.ap[:, 0:8].rearrange("p (h f) -> p h f", f=2)[:, :, half:half + 1].to_broadcast([128, 4, 128]),
                               ALU.mult, [S32.b, glbT.b], [Sd.b])
                            tt("dve", SbfN.ap, v4(Sd), PS[0].ap.rearrange("p (h n) -> p h n", h=4), ALU.add, [Sd.b, PS[0].b], [SbfN.b])
                            tt("dve", S32.ap, v4(Sd), PS[0].ap.rearrange("p (h n) -> p h n", h=4), ALU.add, [Sd.b, PS[0].b], [S32.b])

                    preps, scans, tbs = [], [], []
                    for t4 in range(4):
                        kb.cap_begin()
                        tbs.append(dn_prep(t4))
                        preps.append(kb.cap_end())
                    for t4 in range(4):
                        kb.cap_begin()
                        dn_scan(t4, tbs[t4])
                        scans.append(kb.cap_end())
                    for rec in preps[0]:
                        kb.emit(rec)
                    for t4 in range(4):
                        A_, B_ = scans[t4], (preps[t4 + 1] if t4 < 3 else [])
                        ia = ib = 0
                        while ia < len(A_) or ib < len(B_):
                            if ib < len(B_) and (ia >= len(A_) or ib * len(A_) <= ia * len(B_)):
                                kb.emit(B_[ib])
                                ib += 1
                            else:
                                kb.emit(A_[ia])
                                ia += 1
                    jstate["bank"] = None
                    kb.stage = "DNout"
                    for h in range(4):
                        sqb = T16[h % 2]
                        act(sqb.ap, oraw.ap[:, h, :], AF.Square, [oraw.b], [sqb.b])
                        pn = PS[1 + (h % 2)]
                        mm([(pn.ap, oneb.ap, sqb.ap, True, True)], [oneb.b, sqb.b], [pn.b])
                        rs = T32[h % 2]
                        rsqrt_psum(rs, pn, EPS, rs, scale=1.0 / 128.0)
                        o2 = T32[2 + (h % 2)]
                        tt("pool", o2.ap, oraw.ap[:, h, :], rs.ap, ALU.mult, [oraw.b, rs.b], [o2.b])
                        stt(oT.ap[:, h, :], o2.ap, drv.ap[:, 20:21], dz.ap[:, h, :], ALU.mult, ALU.mult, [o2.b, drv.b, dz.b], [oT.b])

                    kb.stage = "ATTN"
                    wb2 = T(T32[7].ap.bitcast(BF16)[:, 0:512], T32[7].b)
                    acc = T32[0]
                    nb = 4 * c + 4
                    pairs = [(h, bi, b) for h in range(4) for bi, b in enumerate(range(nb - 1, -1, -1))]
                    NP_ = len(pairs)

                    def geo(n):
                        h, bi, b = pairs[n]
                        r = b - 4 * c
                        q0 = max(r, 0) * 128
                        qs = slice(q0, 512)
                        nm = nmask.ap[:, 384:896 - 128 * r] if r >= 0 else None
                        hp = slice((h % 2) * 64, (h % 2) * 64 + 64)
                        hc = h // 2
                        kT_ = sbk.ap[hp, hc, b * 128:(b + 1) * 128]
                        q_ = sbq.ap[hp, hc, qs]
                        return h, bi, b, r, q0, qs, nm, hp, hc, kT_, q_

                    def k0(n):
                        h, bi, b, r, q0, qs, nm, hp, hc, kT_, q_ = geo(n)
                        pz = PS[1 + (n % 2)]
                        specs = [(pz.ap[:, qs], kT_, q_, True, nm is None)]
                        rd = [sbk.b, sbq.b]
                        if nm is not None:
                            specs.append((pz.ap[:, qs], idb.ap, nm, False, True))
                            rd += [idb.b, nmask.b]
                        mm(specs, rd, [pz.b])
                        e_, sp_ = T32[1 + (n % 2)], T32[3 + (n % 4)]
                        act(e_.ap[:, qs], pz.ap[:, qs], AF.Exp, [pz.b], [e_.b])
                        act(sp_.ap[:, qs], e_.ap[:, qs], AF.Ln, [e_.b], [sp_.b], bias=1.0)

                    def k1(n):
                        h, bi, b, r, q0, qs, nm, hp, hc, kT_, q_ = geo(n)
                        sp_ = T32[3 + (n % 4)]
                        spb = T16[n % 3]
                        cp("dve", spb.ap[:, qs], sp_.ap[:, qs], [sp_.b], [spb.b])
                        if b > 0:
                            if bi == 0:
                                if q0 > 0:
                                    kb.op("pool", lambda h_, a=acc.ap: h_.memset(a, 0.0), writes=[acc.b])
                                cp("pool", acc.ap[:, qs], sp_.ap[:, qs], [sp_.b], [acc.b])
                            else:
                                tt("pool", acc.ap[:, qs], acc.ap[:, qs], sp_.ap[:, qs], ALU.add, [acc.b, sp_.b], [acc.b])

                    def k2(n):
                        h, bi, b, r, q0, qs, nm, hp, hc, kT_, q_ = geo(n)
                        spb = T16[n % 3]
                        accb = T16[3 + (n % 2)]
                        accn = T16[3 + ((n + 1) % 2)]
                        pr_ = PS[3 + (n % 2)]
                        specs = [(pr_.ap[:, qs], kT_, q_, True, False),
                                 (pr_.ap[:, qs], ntrib.ap, spb.ap[:, qs], False, bi == 0 and nm is None)]
                        rd = [sbk.b, sbq.b, ntrib.b, spb.b]
                        if bi > 0:
                            specs.append((pr_.ap[:, qs], noneb.ap, accb.ap[:, qs], False, nm is None))
                            rd += [accb.b, noneb.b]
                        if nm is not None:
                            specs.append((pr_.ap[:, qs], idb.ap, nm, False, True))
                            rd += [idb.b, nmask.b]
                        mm(specs, rd, [pr_.b])
                        if b > 0:
                            cp("dve", accn.ap, acc.ap, [acc.b], [accn.b])

                    def k3(n):
                        h, bi, b, r, q0, qs, nm, hp, hc, kT_, q_ = geo(n)
                        e_, sp_ = T32[1 + (n % 2)], T32[3 + (n % 4)]
                        pr_ = PS[3 + (n % 2)]
                        tt("dve", e_.ap[:, qs], pr_.ap[:, qs], sp_.ap[:, qs], ALU.subtract, [pr_.b, sp_.b], [e_.b])
                        wb = T16[5] if n % 2 == 0 else wb2
                        act(wb.ap[:, qs], e_.ap[:, qs], AF.Exp, [e_.b], [wb.b])

                    def k4(n):
                        h, bi, b, r, q0, qs, nm, hp, hc, kT_, q_ = geo(n)
                        wb = T16[5] if n % 2 == 0 else wb2
                        po = PS[0] if h % 2 == 0 else PS[6]
                        mm([(po.ap[:, qs], sbv.ap[:, b, hc * 128:(hc + 1) * 128], wb.ap[:, qs], bi == 0, b == 0)], [sbv.b, wb.b], [po.b], sgc=True)
                        if b == 0:
                            cp("act", oT.ap[hp, 4 + hc, :], po.ap[hp, :], [po.b], [oT.b])

                    stages_ = (k0, k1, k2, k3, k4)
                    for tck in range(NP_ + 4):
                        for kk_ in (4, 3, 2, 1, 0):
                            n = tck - kk_
                            if 0 <= n < NP_:
                                jstate["bank"], jstate["n"] = (PS[5], 1) if kk_ in (0, 2) else (None, 0)
                                stages_[kk_](n)
                    jstate["bank"] = None

                    kb.stage = "A7wout"
                    if c == 0:
                        for cg in range(2):
                            wout_i[cg], wout_t[cg] = ws_next((l, "out", cg))
                    for oc in range(8):
                        ps = PS[1 + (oc % 3)]
                        wt = wout_t[oc // 4]
                        mm([(ps.ap, wt.ap[:, k, (oc % 4) * 128:(oc % 4 + 1) * 128], oT.ap[:, k, :], k == 0, k == 7) for k in range(8)], [wt.b, oT.b], [ps.b])
                        tt("dve", xT.ap[:, oc, c0:c0 + 512], xT.ap[:, oc, c0:c0 + 512], ps.ap, ALU.add, [xT.b, ps.b], [xT.b])
                for cg in range(2):
                    ws_release(wout_i[cg])

                conv_upto(l, "f2")
                kb.stage = "Fnorm"
                kb.barrier()
                ar = Arena(SCR0, MEMB)
                pl = ar.a("plF", NPL, F32)
                drv = ar.a("drvF", 64, F32)
                h2 = ar.a("h2", 8 * S, BF16, (8, S))
                aT = ar.a("aT", 8 * 512, BF16, (8, 512))
                aT2 = ar.a("aT2", 8 * 512, BF16, (8, 512))
                T32 = [ar.a("t32F_%d" % i, 512, F32) for i in range(4)]
                kb.dma("sp", pl.ap, pl_d[l], pl.b, writes=[pl.b])
                ts("dve", drv.ap[:, 0:16], pl.ap[:, P_G1:P_G1 + 16], 32.0, None, ALU.mult, None, [pl.b], [drv.b])
                for c in range(NCH):
                    c0 = c * 512
                    act(aT.ap, xT.ap[:, :, c0:c0 + 512], AF.Square, [xT.b], [aT.b])
                    mm([(PS[0].ap, oneb.ap, aT.ap[:, k, :], k == 0, k == 7) for k in range(8)], [oneb.b, aT.b], [PS[0].b])
                    rsqrt_psum(T32[0], PS[0], 1024.0 * EPS, T32[1])
                    for k in range(8):
                        stt(h2.ap[:, k, c0:c0 + 512], xT.ap[:, k, c0:c0 + 512], drv.ap[:, 8 + k:9 + k], T32[0].ap,
                            ALU.mult, ALU.mult, [xT.b, drv.b, T32[0].b], [h2.b])
                kb.stage = "FFN"
                aTs = [aT, aT2]
                work = [(fg, c) for fg in range(4) for c in range(NCH)]
                held = {}

                def ff1(n, fg, c):
                    if c == 0:
                        conv_next(3)
                        if fg == 3 and l + 1 < L:
                            conv_upto(l + 1, "out")
                        held[fg] = [ws_next((l, "f1", 2 * fg)), ws_next((l, "f1", 2 * fg + 1))]
                    w1a, w1b = held[fg][0][1], held[fg][1][1]
                    c0 = c * 512
                    at = aTs[n % 2]
                    for f in range(8):
                        w1 = w1a if f < 4 else w1b
                        ps = PS[f % 3]
                        mm([(ps.ap, w1.ap[:, k, (f % 4) * 128:(f % 4 + 1) * 128], h2.ap[:, k, c0:c0 + 512], k == 0, k == 7) for k in range(8)],
                           [w1.b, h2.b], [ps.b])
                        rl = T32[2 + (f % 2)]
                        act(rl.ap, ps.ap, AF.Relu, [ps.b], [rl.b])
                        act(at.ap[:, f, :], rl.ap, AF.Square, [rl.b], [at.b])
                    if c == NCH - 1:
                        ws_release(held[fg][0][0])
                        ws_release(held[fg][1][0])

                def ff2(n, fg, c):
                    if c == 0:
                        held[fg] += [ws_next((l, "f2", fg, 0)), ws_next((l, "f2", fg, 1))]
                    w2a, w2b = held[fg][2][1], held[fg][3][1]
                    c0 = c * 512
                    at = aTs[n % 2]
                    for oc in range(8):
                        w2 = w2a if oc < 4 else w2b
                        ps = PS[3 + (oc % 3)]
                        mm([(ps.ap, w2.ap[:, k, (oc % 4) * 128:(oc % 4 + 1) * 128], at.ap[:, k, :], k == 0, k == 7) for k in range(8)],
                           [w2.b, at.b], [ps.b])
                        tt("dve", xT.ap[:, oc, c0:c0 + 512], xT.ap[:, oc, c0:c0 + 512], ps.ap, ALU.add, [xT.b, ps.b], [xT.b])
                    if c == NCH - 1:
                        for (i_, _t) in held.pop(fg)[2:]:
                            ws_release(i_)

                prevw = None
                for n, wk in enumerate(work + [None]):
                    if wk is not None:
                        ff1(n, *wk)
                    if prevw is not None:
                        ff2(n - 1, *prevw)
                    prevw = wk

            kb.stage = "store"
            kb.barrier()
            ar = Arena(SCR0, MEMB)
            yo = [ar.a("yo%d" % i, 1024, F32) for i in range(2)]
            for t in range(NT):
                yt = yo[t % 2]
                for hf in range(2):
                    ps = PS[(2 * t + hf) % 4]
                    tr([(ps.ap[:, j * 128:(j + 1) * 128], xT.ap[:, hf * 4 + j, t * 128:(t + 1) * 128], ident) for j in range(4)],
                       [xT.b, cst.b], [ps.b])
                    cp("act" if hf else "dve", yt.ap[:, hf * 512:(hf + 1) * 512], ps.ap, [ps.b], [yt.b])
                kb.dma("sp", y_d[seq, t * 128:(t + 1) * 128, :], yt.ap, yt.b, reads=[yt.b])

        with nc.Block() as block:
            kb.finish(block)
        print("prog sizes:", {n: len(e.prog) for n, e in kb.eng.items()}, "sems:", len(kb.sems))
        global LAST_LABELS
        LAST_LABELS = {n: list(e.labels) for n, e in kb.eng.items()}
    return nc


LAST_LABELS = None


_NC_CACHE = {}


def run(inputs, NSEQ, S, L, n_cores):
    key = (NSEQ, S, L)
    if key not in _NC_CACHE:
        _NC_CACHE[key] = build(NSEQ, S, L)
    nc = _NC_CACHE[key]
    f32 = lambda a: np.ascontiguousarray(np.asarray(a, dtype=np.float32))
    inp = {k: f32(v) for k, v in inputs.items()}
    pl = make_params(inp, L)
    cst = make_consts()
    x = inp["x"]
    in_maps = []
    for ci in range(n_cores):
        in_maps.append({
            "x": np.ascontiguousarray(x[ci * NSEQ:(ci + 1) * NSEQ]),
            "w_in": inp["w_in"], "w_out": inp["w_out"], "w_ff1": inp["w_ff1"], "w_ff2": inp["w_ff2"],
            "pl": pl, "cst": cst,
        })
    res = run_bass_kernel_spmd(nc, in_maps, core_ids=list(range(n_cores)))
    return np.concatenate([np.asarray(r["y"]) for r in res.results], axis=0)


def kernel(**inputs):
    x = np.asarray(inputs["x"])
    B, S, D = x.shape
    L = np.asarray(inputs["w_in"]).shape[0]
    NSEQ = B // N_CORES
    return run(inputs, NSEQ, S, L, N_CORES).astype(np.float32)
```

```python
import numpy as np
import concourse.bass as bass
import concourse.mybir as mybir
from concourse.bass_utils import run_bass_kernel_spmd
from contextlib import ExitStack

F32 = mybir.dt.float32
BF16 = mybir.dt.bfloat16
AF = mybir.ActivationFunctionType
ALU = mybir.AluOpType
AX = mybir.AxisListType

EPS = 1e-6
N_CORES = 8


class Buf:
    __slots__ = ("name", "w", "r", "excl", "dsem", "dcnt")

    def __init__(self, name, excl=False):
        self.name = name
        self.w = None
        self.r = []
        self.excl = excl
        self.dsem = None
        self.dcnt = 0


class Eng:
    def __init__(self, name, sem):
        self.name = name
        self.sem = sem
        self.cnt = 0
        self.known = {}
        self.prog = []
        self.labels = []
        self.hist = []


class KB:
    def __init__(self, nc, stack):
        self.nc = nc
        self.stack = stack
        self.sems = {}
        self.eng = {}
        self.dbufs = []
        self.stage = "pre"
        self.capture = None
        for n in ("pe", "act", "dve", "pool", "sp"):
            s = stack.enter_context(nc.semaphore("s_" + n))
            self.sems["e_" + n] = s
            self.eng[n] = Eng(n, "e_" + n)
        self.by_sem = {e.sem: e for e in self.eng.values()}

    def _deps(self, e, reads, writes):
        deps = {}

        def add(ev):
            if ev is None:
                return
            k, v, en = ev
            if en == "pe" and e.name == "pe":
                return
            if deps.get(k, 0) < v:
                deps[k] = v

        for b in reads:
            add(b.w)
            if b.excl:
                for ev in b.r:
                    if ev[2] != e.name:
                        add(ev)
        for b in writes:
            add(b.w)
            for ev in b.r:
                add(ev)
        waits = []
        snaps = []
        for k, v in sorted(deps.items(), key=lambda kv: -kv[1]):
            if e.known.get(k, 0) >= v:
                continue
            e.known[k] = v
            waits.append((k, v))
            src = self.by_sem.get(k)
            snap = src.hist[v - 1] if src is not None else None
            snaps.append(snap)
            if snap:
                for kk, vv in snap.items():
                    if e.known.get(kk, 0) < vv:
                        e.known[kk] = vv
        if len(waits) > 1:
            keep = []
            for i, (k, v) in enumerate(waits):
                implied = any(sn is not None and sn.get(k, 0) >= v for j, sn in enumerate(snaps) if j != i)
                if not implied:
                    keep.append((k, v))
            waits = keep
        return waits

    def op(self, en, fns, reads=(), writes=()):
        if self.capture is not None:
            self.capture.append((en, fns, tuple(reads), tuple(writes), self.stage))
            return
        e = self.eng[en]
        if callable(fns):
            fns = [fns]
        waits = self._deps(e, reads, writes)
        e.cnt += 1
        e.hist.append(dict(e.known))
        ev = (e.sem, e.cnt, e.name)
        for i, fn in enumerate(fns):
            last = i == len(fns) - 1
            e.prog.append((waits if i == 0 else [], fn, (e.sem, 1) if last else None))
            e.labels.append(self.stage)
        for b in reads:
            b.r.append(ev)
        for b in writes:
            b.w = ev
            b.r = []

    def cap_begin(self):
        self.capture = []

    def cap_end(self):
        c, self.capture = self.capture, None
        return c

    def emit(self, rec):
        en, fns, r, w, st = rec
        self.stage = st
        self.op(en, fns, r, w)

    def dma(self, en, out, in_, dbuf, reads=(), writes=()):
        e = self.eng[en]
        if dbuf.dsem is None:
            key = "d_%d" % len(self.sems)
            self.sems[key] = self.stack.enter_context(self.nc.semaphore(key))
            dbuf.dsem = key
            self.dbufs.append(dbuf)
        waits = self._deps(e, reads, writes)
        dbuf.dcnt += 16
        ev = (dbuf.dsem, dbuf.dcnt, "dma")
        e.prog.append((waits, (lambda h, o=out, i=in_: h.dma_start(out=o, in_=i)), (dbuf.dsem, 16)))
        e.labels.append("dma:" + self.stage)
        for b in reads:
            b.r.append(ev)
        for b in writes:
            b.w = ev
            b.r = []

    def barrier(self):
        evs = [(e.sem, e.cnt) for e in self.eng.values() if e.cnt > 0]
        evs += [(b.dsem, b.dcnt) for b in self.dbufs]
        for e in self.eng.values():
            waits = []
            for k, v in evs:
                if e.known.get(k, 0) < v:
                    e.known[k] = v
                    waits.append((k, v))
            if waits:
                e.prog.append((waits, None, None))

    def finish(self, block):
        self.barrier()
        sems = self.sems
        engs = self.eng

        def replay(h, name):
            for waits, fn, inc in engs[name].prog:
                for k, v in waits:
                    h.wait_ge(sems[k], v)
                if fn is None:
                    continue
                ins = fn(h)
                if inc is not None:
                    ins.then_inc(sems[inc[0]], inc[1])

        @block.tensor
        def _(h):
            replay(h, "pe")

        @block.scalar
        def _(h):
            replay(h, "act")

        @block.vector
        def _(h):
            replay(h, "dve")

        @block.gpsimd
        def _(h):
            replay(h, "pool")

        @block.sync
        def _(h):
            replay(h, "sp")


class T:
    __slots__ = ("ap", "b")

    def __init__(self, ap, b):
        self.ap = ap
        self.b = b


C_ID, C_TRI, C_BLK, C_MI, C_MS, C_MIT, C_MW, C_NTRI, C_NH, NCST = 0, 128, 256, 384, 512, 640, 768, 1664, 1792, 1800
P_G1, P_G2, P_CONV, P_ALOG, P_DTB, P_DNG, P_SBQ, P_SBK, P_SGB, P_SGVG, P_SGWT, NPL = 0, 8, 16, 64, 68, 72, 73, 74, 76, 80, 336, 848


def make_consts():
    c = np.zeros((128, NCST), np.float32)
    i = np.arange(128)
    c[:, C_ID:C_ID + 128] = np.eye(128)
    same = (i[:, None] // 64) == (i[None, :] // 64)
    c[:, C_TRI:C_TRI + 128] = same & (i[:, None] <= i[None, :])
    c[:, C_BLK:C_BLK + 128] = same
    c[:, C_MI:C_MI + 128] = same & (i[None, :] <= i[:, None])
    c[:, C_MS:C_MS + 128] = same & (i[None, :] < i[:, None])
    c[:, C_MIT:C_MIT + 128] = i[None, :] >= i[:, None]
    u = np.arange(896)
    c[:, C_MW:C_MW + 896] = (u[None, :] - 384) > i[:, None]
    c[:, C_NTRI:C_NTRI + 128] = -1.0 * (i[:, None] > i[None, :])
    c[:, C_NH] = -0.5
    return c


def make_params(inp, L):
    pl = np.zeros((L, 128, NPL), np.float32)
    for l in range(L):
        pl[l, :, P_G1:P_G1 + 8] = inp["norm1_g"][l].reshape(8, 128).T
        pl[l, :, P_G2:P_G2 + 8] = inp["norm2_g"][l].reshape(8, 128).T
        pl[l, :, P_CONV:P_CONV + 48] = inp["conv_w"][l].reshape(4, 12, 128).transpose(2, 1, 0).reshape(128, 48)
        pl[l, :, P_ALOG:P_ALOG + 4] = inp["a_log"][l][None, :]
        pl[l, :, P_DTB:P_DTB + 4] = inp["dt_bias"][l][None, :]
        pl[l, :, P_DNG] = inp["dn_out_g"][l]
        pl[l, :, P_SBQ] = np.tile(inp["sb_q_g"][l], 2)
        pl[l, :, P_SBK] = np.tile(inp["sb_k_g"][l], 2)
        pl[l, :, P_SGB:P_SGB + 4] = inp["sg_b"][l].T
        pl[l, :, P_SGVG:P_SGVG + 256] = inp["sg_v_g"][l][None, :]
        pl[l, :, P_SGWT:P_SGWT + 512] = inp["sg_w"][l].transpose(2, 0, 1).reshape(128, 512)
    return pl


WIN_SLICES = [
    [(0, 512)], [(512, 512)], [(1024, 512)], [(1536, 512)],
    [(2056, 512)],
    [(2568, 256), (2048, 8)],
    [(2824, 512)],
]


WIN_ORDER = [0, 1, 2, 3, 5, 6, 4]
JUNK = True


def build(NSEQ, S, L, dbg=False):
    NCH = S // 512
    NT = S // 128
    nc = bass.Bass("TRN2", target_bir_lowering=False)
    x_d = nc.dram_tensor("x", [NSEQ, S, 1024], F32, kind="ExternalInput").ap()
    win_d = nc.dram_tensor("w_in", [L, 1024, 3336], F32, kind="ExternalInput").ap()
    wout_d = nc.dram_tensor("w_out", [L, 1024, 1024], F32, kind="ExternalInput").ap()
    wf1_d = nc.dram_tensor("w_ff1", [L, 1024, 4096], F32, kind="ExternalInput").ap()
    wf2_d = nc.dram_tensor("w_ff2", [L, 4096, 1024], F32, kind="ExternalInput").ap()
    pl_d = nc.dram_tensor("pl", [L, 128, NPL], F32, kind="ExternalInput").ap()
    cst_d = nc.dram_tensor("cst", [128, NCST], F32, kind="ExternalInput").ap()
    y_d = nc.dram_tensor("y", [NSEQ, S, 1024], F32, kind="ExternalOutput").ap()
    dbg_d = nc.dram_tensor("dbg", [128, 8, 512], F32, kind="ExternalOutput").ap() if dbg else None

    st = ExitStack()
    with st:
        kb = KB(nc, st)
        MEMB = 212480
        M = st.enter_context(nc.sbuf_tensor("M", [128, MEMB // 4], F32))

        class Arena:
            def __init__(self, base, limit):
                self.off = base
                self.limit = limit

            def a(self, name, nelem, dtype, shape=None):
                nbytes = nelem * (4 if dtype == F32 else 2)
                nbytes = (nbytes + 63) // 64 * 64
                o = self.off
                self.off += nbytes
                assert self.off <= self.limit, (name, self.off, self.limit)
                ap = M[:, o // 4:(o + nbytes) // 4]
                if dtype == BF16:
                    ap = ap.bitcast(BF16)
                ap = ap[:, 0:nelem]
                if shape is not None:
                    if len(shape) == 2:
                        ap = ap.rearrange("p (a b) -> p a b", a=shape[0])
                    elif len(shape) == 3:
                        ap = ap.rearrange("p (a b c) -> p a b c", a=shape[0], b=shape[1])
                return T(ap, Buf(name))

        fixed = Arena(0, MEMB)
        xT = fixed.a("xT", 8 * S, F32, (8, S))
        NSLOT = 5
        slots = [fixed.a("slot%d" % i, 8 * 512, BF16, (8, 512)) for i in range(NSLOT)]
        cst = fixed.a("cst", NCST, F32)
        idb = fixed.a("idb", 128, BF16)
        oneb = fixed.a("oneb", 128, BF16)
        blkb = fixed.a("blkb", 128, BF16)
        ntrib = fixed.a("ntrib", 128, BF16)
        noneb = fixed.a("noneb", 128, BF16)
        nmask = fixed.a("nmask", 896, BF16)
        SCR0 = fixed.off

        PS = []
        for i in range(7):
            t = st.enter_context(nc.psum_tensor("ps%d" % i, [128, 512], F32))
            PS.append(T(t[:], Buf("ps%d" % i, excl=True)))
        tpb = st.enter_context(nc.psum_tensor("psb", [128, 1024], BF16))
        PB = T(tpb[:], Buf("psb", excl=True))

        def cs(off, n=128):
            return cst.ap[:, off:off + n]

        ident = cs(C_ID)
        def act(out, in_, func, r, w, **kw):
            kb.op("act", lambda h: h.activation(out=out, in_=in_, func=func, **kw), reads=r, writes=w)

        def tt(en, out, in0, in1, op, r, w):
            kb.op(en, lambda h: h.tensor_tensor(out=out, in0=in0, in1=in1, op=op), reads=r, writes=w)

        def ts(en, out, in0, s1, s2, op0, op1, r, w):
            if s2 is None:
                kb.op(en, lambda h: h.tensor_scalar(out=out, in0=in0, scalar1=s1, scalar2=None, op0=op0), reads=r, writes=w)
            else:
                kb.op(en, lambda h: h.tensor_scalar(out=out, in0=in0, scalar1=s1, scalar2=s2, op0=op0, op1=op1), reads=r, writes=w)

        def stt(out, in0, sc, in1, op0, op1, r, w):
            kb.op("dve", lambda h: h.scalar_tensor_tensor(out=out, in0=in0, scalar=sc, in1=in1, op0=op0, op1=op1), reads=r, writes=w)

        def cp(en, out, in_, r, w):
            if en == "act":
                act(out, in_, AF.Copy, r, w)
            else:
                kb.op(en, lambda h: h.tensor_copy(out=out, in_=in_), reads=r, writes=w)

        def mm(specs, r, w, sgc=False):
            if sgc:
                fns = [(lambda h, o=o, l=l, rr=rr, s=s, e=e: h.matmul(o, lhsT=l, rhs=rr, start=s, stop=e, skip_group_check=True)) for (o, l, rr, s, e) in specs]
            else:
                fns = [(lambda h, o=o, l=l, rr=rr, s=s, e=e: h.matmul(o, lhsT=l, rhs=rr, start=s, stop=e)) for (o, l, rr, s, e) in specs]
            kb.op("pe", fns, reads=r, writes=w)
            junk()

        def tr(specs, r, w):
            fns = [(lambda h, o=o, i=i, d=d: h.transpose(o, i, d)) for (o, i, d) in specs]
            kb.op("pe", fns, reads=r, writes=w)
            junk()

        jstate = {"bank": None, "n": 0}

        def junk():
            jb = jstate["bank"]
            if not JUNK or jb is None or jstate["n"] <= 0:
                return
            fns = [(lambda h, o=jb.ap: h.matmul(o, lhsT=idb.ap, rhs=nmask.ap[:, 0:512], start=True, stop=True)) for _ in range(jstate["n"])]
            kb.op("pe", fns, reads=[idb.b, nmask.b], writes=[jb.b])

        def rsqrt_psum(out_t, ps_t, addc, tmp_t, cols=512, scale=1.0):
            act(tmp_t.ap[:, 0:cols], ps_t.ap[:, 0:cols], AF.Ln, [ps_t.b], [tmp_t.b], bias=addc, scale=scale)
            act(out_t.ap[:, 0:cols], tmp_t.ap[:, 0:cols], AF.Exp, [tmp_t.b], [out_t.b], scale=-0.5)

        kb.dma("sp", cst.ap, cst_d, cst.b, writes=[cst.b])
        cp("dve", idb.ap, ident, [cst.b], [idb.b])
        cp("dve", blkb.ap, cs(C_BLK), [cst.b], [blkb.b])
        cp("dve", ntrib.ap, cs(C_NTRI), [cst.b], [ntrib.b])
        kb.op("pool", lambda h: h.memset(oneb.ap, 1.0), writes=[oneb.b])
        kb.op("pool", lambda h: h.memset(noneb.ap, -1.0), writes=[noneb.b])
        ts("dve", nmask.ap, cst.ap[:, C_MW:C_MW + 896], -1.0, 30000.0, ALU.add, ALU.mult, [cst.b], [nmask.b])

        wsc = {}

        def conv_block(key, ncols, pieces):
            th = nc.dram_tensor("wsc_%s" % "_".join(str(k) for k in key), [1024, ncols], BF16)
            ap = th.ap()
            b = Buf("wsc")
            c0 = 0
            for (src, n) in pieces:
                kb.dma("pool", ap[:, c0:c0 + n], src, b, writes=[b])
                c0 += n
            wsc[key] = (ap.rearrange("(k p) n -> p k n", p=128), b, ncols)

        conv_pending = []
        for l in range(L):
            for si, pieces in enumerate(WIN_SLICES):
                ncols = sum(n for _, n in pieces)
                conv_pending.append(((l, "in", si), ncols, [(win_d[l, :, c0:c0 + n], n) for (c0, n) in pieces]))
            for cg in range(2):
                conv_pending.append(((l, "out", cg), 512, [(wout_d[l, :, cg * 512:(cg + 1) * 512], 512)]))
            for cg in range(8):
                conv_pending.append(((l, "f1", cg), 512, [(wf1_d[l, :, cg * 512:(cg + 1) * 512], 512)]))
            for rg in range(4):
                for cg in range(2):
                    conv_pending.append(((l, "f2", rg, cg), 512, [(wf2_d[l, rg * 1024:(rg + 1) * 1024, cg * 512:(cg + 1) * 512], 512)]))

        def conv_next(n=1):
            for _ in range(n):
                if conv_pending:
                    conv_block(*conv_pending.pop(0))

        def conv_upto(l, name):
            idx = [i for i, (k, _, _) in enumerate(conv_pending) if k[0] == l and k[1] == name]
            if idx:
                conv_next(idx[-1] + 1)

        conv_upto(0, "out")

        sched = []
        for s in range(NSEQ):
            for l in range(L):
                for c in range(NCH):
                    for si in WIN_ORDER:
                        sched.append((l, "in", si))
                    if c == 0:
                        sched.append((l, "out", 0))
                        sched.append((l, "out", 1))
                for fg in range(4):
                    sched += [(l, "f1", 2 * fg), (l, "f1", 2 * fg + 1), (l, "f2", fg, 0), (l, "f2", fg, 1)]
        ws_state = {"pos": 0, "use": 0, "free": list(range(NSLOT)), "loaded": {}}

        def ws_fill():
            while ws_state["free"] and ws_state["pos"] < len(sched):
                sl = ws_state["free"].pop(0)
                key = sched[ws_state["pos"]]
                src, sb, ncols = wsc[key]
                kb.dma("sp", slots[sl].ap[:, :, 0:ncols], src, slots[sl].b, reads=[sb], writes=[slots[sl].b])
                ws_state["loaded"][ws_state["pos"]] = sl
                ws_state["pos"] += 1

        def ws_next(key):
            i = ws_state["use"]
            assert sched[i] == key, (sched[i], key)
            ws_state["use"] += 1
            assert i in ws_state["loaded"], "slice not prefetched (deadlock): %s" % (key,)
            return i, slots[ws_state["loaded"][i]]

        def ws_release(i):
            ws_state["free"].append(ws_state["loaded"].pop(i))
            ws_fill()

        for seq in range(NSEQ):
            kb.stage = "xload"
            kb.barrier()
            ar = Arena(SCR0, MEMB)
            xin = [ar.a("xin%d" % i, 1024, F32) for i in range(2)]
            for t in range(NT):
                xi = xin[t % 2]
                kb.dma("sp", xi.ap, x_d[seq, t * 128:(t + 1) * 128, :], xi.b, writes=[xi.b])
                for hf in range(2):
                    ps = PS[(2 * t + hf) % 4]
                    tr([(ps.ap[:, j * 128:(j + 1) * 128], xi.ap[:, (hf * 4 + j) * 128:(hf * 4 + j + 1) * 128], ident) for j in range(4)],
                       [xi.b, cst.b], [ps.b])
                    cp("act" if hf else "dve", xT.ap[:, hf * 4:hf * 4 + 4, t * 128:(t + 1) * 128],
                       ps.ap.rearrange("p (a b) -> p a b", a=4), [ps.b], [xT.b])

            if seq == 0:
                ws_fill()
            for l in range(L):
                kb.stage = "Asetup"
                kb.barrier()
                ar = Arena(SCR0, MEMB)
                pl = ar.a("pl", NPL, F32)
                drv = ar.a("drv", 64, F32)
                sgwm = ar.a("sgwm", 512, BF16, (4, 128))
                diag = [ar.a("diag%d" % i, 512, BF16, (4, 128)) for i in range(2)]
                hT = ar.a("hT", 8 * 512, BF16, (8, 512))
                oT = hT
                dq = ar.a("dq", 4 * 512, BF16, (4, 512))
                dk = ar.a("dk", 4 * 512, BF16, (4, 512))
                dv = ar.a("dv", 4 * 512, BF16, (4, 512))
                dz = ar.a("dz", 4 * 512, BF16, (4, 512))
                sbq = ar.a("sbq", 2 * 512, BF16, (2, 512))
                sbk = ar.a("sbk", 2 * S, BF16, (2, S))
                sbv = ar.a("sbv", NT * 256, BF16, (NT, 256))
                sgu = ar.a("sgu", 4 * 256, BF16, (4, 256))
                sgv = ar.a("sgv", 4 * 256, BF16, (4, 256))
                oraw = ar.a("oraw", 4 * 512, BF16, (4, 512))
                pre = [ar.a("pre%d" % i, 520, BF16) for i in range(2)]
                halo = ar.a("halo", 12 * 4, BF16, (12, 4))
                S32 = ar.a("S32", 512, F32, (4, 128))
                Sbfs = [ar.a("Sbf%d" % i, 512, BF16, (4, 128)) for i in range(2)]
                sm = ar.a("sm", 16 * 16, F32, (16, 16))
                gg = ar.a("gg", 32, F32, (4, 8))
                glb = ar.a("glb", 32, F32)
                T32 = [ar.a("t32_%d" % i, 512, F32) for i in range(8)]
                T16 = [ar.a("t16_%d" % i, 512, BF16) for i in range(6)]
                _su = sgu.ap.rearrange("p a b -> p (a b)")
                _sv = sgv.ap.rearrange("p a b -> p (a b)")
                T16b = [T(_su[:, 0:512], sgu.b), T(_su[:, 512:1024], sgu.b), T(_sv[:, 0:512], sgv.b), T(_sv[:, 512:1024], sgv.b)]
                Upar = [ar.a("Upar%d" % i, 512, F32) for i in range(2)]
                SdT = ar.a("SdT", 512, F32)
                glb2 = ar.a("glb2", 32, F32)

                kb.dma("sp", pl.ap, pl_d[l], pl.b, writes=[pl.b])
                ts("dve", drv.ap[:, 0:16], pl.ap[:, P_G1:P_G1 + 16], 32.0, None, ALU.mult, None, [pl.b], [drv.b])
                act(drv.ap[:, 16:20], pl.ap[:, P_ALOG:P_ALOG + 4], AF.Exp, [pl.b], [drv.b])
                ts("dve", drv.ap[:, 16:20], drv.ap[:, 16:20], -1.0, None, ALU.mult, None, [drv.b], [drv.b])
                ts("dve", drv.ap[:, 20:21], pl.ap[:, P_DNG:P_DNG + 1], 0.5, None, ALU.mult, None, [pl.b], [drv.b])
                ts("dve", drv.ap[:, 21:22], pl.ap[:, P_SBQ:P_SBQ + 1], 1.0, None, ALU.mult, None, [pl.b], [drv.b])
                ts("dve", drv.ap[:, 22:23], pl.ap[:, P_SBK:P_SBK + 1], 8.0, None, ALU.mult, None, [pl.b], [drv.b])
                ts("dve", drv.ap[:, 24:28], pl.ap[:, P_SGB:P_SGB + 4], 0.5, None, ALU.mult, None, [pl.b], [drv.b])
                stt(sgwm.ap, pl.ap[:, P_SGWT:P_SGWT + 512].rearrange("p (g t) -> p g t", g=4), 0.5,
                    cs(C_MIT).unsqueeze(1).to_broadcast([128, 4, 128]), ALU.mult, ALU.mult, [pl.b, cst.b], [sgwm.b])
                kb.op("pool", lambda h: h.memset(S32.ap, 0.0), writes=[S32.b])
                kb.op("pool", lambda h: h.memset(Sbfs[0].ap, 0.0), writes=[Sbfs[0].b])
                kb.op("pool", lambda h: h.memset(halo.ap, 0.0), writes=[halo.b])

                wout_i = [None, None]
                wout_t = [None, None]

                for c in range(NCH):
                    c0 = c * 512
                    conv_next(4 if NCH >= 4 else 16)
                    if c == NCH - 1:
                        conv_upto(l, "f2")
                    kb.stage = "A1norm"
                    def rmsnorm_chunk(gcol, dst, c0=c0):
                        act(dst.ap, xT.ap[:, :, c0:c0 + 512], AF.Square, [xT.b], [dst.b])
                        mm([(PS[0].ap, oneb.ap, dst.ap[:, k, :], k == 0, k == 7) for k in range(8)], [oneb.b, dst.b], [PS[0].b])
                        rsqrt_psum(T32[0], PS[0], 1024.0 * EPS, T32[1])
                        for k in range(8):
                            stt(dst.ap[:, k, :], xT.ap[:, k, c0:c0 + 512], drv.ap[:, gcol + k:gcol + k + 1], T32[0].ap,
                                ALU.mult, ALU.mult, [xT.b, drv.b, T32[0].b], [dst.b])
                    rmsnorm_chunk(0, hT)

                    kb.stage = "A2dnproj"
                    def proj_fm(slot, j, ps):
                        mm([(ps.ap, slot.ap[:, k, j * 128:(j + 1) * 128], hT.ap[:, k, :], k == 0, k == 7) for k in range(8)],
                           [slot.b, hT.b], [ps.b])

                    items = [(grp, hh) for grp in range(4) for hh in range(4)]
                    dsts = (dq, dk, dv, dz)
                    cur_w = {}

                    def emit_proj(grp, hh):
                        if hh == 0:
                            cur_w["w"] = ws_next((l, "in", grp))
                        wi, wsl = cur_w["w"]
                        j = grp * 4 + hh
                        ps = PS[1 + (j % 2)]
                        proj_fm(wsl, hh, ps)
                        if hh == 3:
                            ws_release(wi)
                        if grp < 3:
                            pr = pre[j % 2]
                            dg = diag[j % 2]
                            cp("pool", pr.ap[:, 0:3], halo.ap[:, j, 0:3], [halo.b], [pr.b])
                            cp("act", pr.ap[:, 3:515], ps.ap, [ps.b], [pr.b])
                            cp("pool", halo.ap[:, j, 0:3], pr.ap[:, 512:515], [pr.b], [halo.b])
                            for i in range(4):
                                act(dg.ap[:, i, :], idb.ap, AF.Copy, [idb.b, pl.b], [dg.b], scale=pl.ap[:, P_CONV + j * 4 + i:P_CONV + j * 4 + i + 1])

                    def emit_conv(grp, hh):
                        j = grp * 4 + hh
                        dst = dsts[grp]
                        if grp < 3:
                            pr = pre[j % 2]
                            dg = diag[j % 2]
                            pc = PS[3 + (j % 2)]
                            mm([(pc.ap, dg.ap[:, i, :], pr.ap[:, i:i + 512], i == 0, i == 3) for i in range(4)], [dg.b, pr.b], [pc.b])
                        else:
                            pc = PS[1 + (j % 2)]
                        th = T32[2 + (j % 2)]
                        act(th.ap, pc.ap, AF.Tanh, [pc.b], [th.b], scale=0.5)
                        stt(dst.ap[:, hh, :], th.ap, 1.0, pc.ap, ALU.add, ALU.mult, [th.b, pc.b], [dst.b])

                    prev = None
                    for it in items + [None]:
                        if it is not None:
                            emit_proj(*it)
                        if prev is not None:
                            emit_conv(*prev)
                        prev = it
                    kb.stage = "A3tok"
                    wi5, w5 = ws_next((l, "in", 5))
                    wi6, w6 = ws_next((l, "in", 6))
                    ab = sm.ap[:, 0:2, :].rearrange("p a (t e) -> p (a t) e", e=8)
                    def a3x(t4):
                        tg = c * 4 + t4
                        tc = slice(t4 * 128, (t4 + 1) * 128)
                        pa = PS[1 + (t4 % 2)]
                        mm([(pa.ap[:, 0:264], hT.ap[:, k, tc], w5.ap[:, k, 0:264], k == 0, k == 7) for k in range(8)], [hT.b, w5.b], [pa.b])
                        cp("act", sbv.ap[:, tg, :], pa.ap[:, 0:256], [pa.b], [sbv.b])
                        cp("dve", ab[:, t4, :], pa.ap[:, 256:264], [pa.b], [sm.b])
                        pg = PS[3 + (t4 % 2)]
                        mm([(pg.ap, hT.ap[:, k, tc], w6.ap[:, k, :], k == 0, k == 7) for k in range(8)], [hT.b, w6.b], [pg.b])
                        xs, t1, t2 = T32[0 + 3 * (t4 % 2)], T32[1 + 3 * (t4 % 2)], T32[2 + 3 * (t4 % 2)]
                        cp("act", xs.ap, pg.ap, [pg.b], [xs.b])
                        act(t1.ap, pg.ap, AF.Square, [pg.b], [t1.b])

                    def a3y(t4):
                        xs, t1, t2 = T32[0 + 3 * (t4 % 2)], T32[1 + 3 * (t4 % 2)], T32[2 + 3 * (t4 % 2)]
                        ts("dve", t1.ap, t1.ap, 0.044715, 1.0, ALU.mult, ALU.add, [t1.b], [t1.b])
                        tt("dve", t1.ap, t1.ap, xs.ap, ALU.mult, [t1.b, xs.b], [t1.b])
                        act(t2.ap, t1.ap, AF.Tanh, [t1.b], [t2.b], scale=0.7978845608028654)
                        stt(sgu.ap[:, t4, :], t2.ap[:, 0:256], 1.0, xs.ap[:, 0:256], ALU.add, ALU.mult, [t2.b, xs.b], [sgu.b])
                        stt(t1.ap[:, 256:512], t2.ap[:, 256:512], 1.0, xs.ap[:, 256:512], ALU.add, ALU.mult, [t2.b, xs.b], [t1.b])
                        act(t2.ap[:, 256:512], t1.ap[:, 256:512], AF.Square, [t1.b], [t2.b])
                        ssr = sm.ap[:, 2 + (t4 % 2), 0:4]
                        kb.op("dve", lambda h, o=ssr, i=t2.ap[:, 256:512].rearrange("p (g d) -> p g d", g=4): h.tensor_reduce(out=o, in_=i, axis=AX.X, op=ALU.add),
                              reads=[t2.b], writes=[sm.b])
                        ts("dve", ssr, ssr, 1.0 / 64.0, 4.0 * EPS, ALU.mult, ALU.add, [sm.b], [sm.b])
                        tt("pool", ssr, ssr, cst.ap[:, C_NH:C_NH + 1].to_broadcast([128, 4]), ALU.pow, [sm.b, cst.b], [sm.b])
                        tt("dve", t1.ap[:, 256:512].rearrange("p (g d) -> p g d", g=4), t1.ap[:, 256:512].rearrange("p (g d) -> p g d", g=4),
                           ssr.unsqueeze(2).to_broadcast([128, 4, 64]), ALU.mult, [t1.b, sm.b], [t1.b])
                        tt("pool", sgv.ap[:, t4, :], t1.ap[:, 256:512], pl.ap[:, P_SGVG:P_SGVG + 256], ALU.mult, [t1.b, pl.b], [sgv.b])

                    for t4 in range(5):
                        if t4 < 4:
                            a3x(t4)
                        if t4 >= 1:
                            a3y(t4 - 1)
                    ws_release(wi5)
                    ws_release(wi6)

                    kb.stage = "l2n_sbproj"
                    l2items = [("dn", grp, hh) for grp in range(2) for hh in range(4)] + [("sb", j, 0) for j in range(4)]
                    sbw = {}

                    PNB = (PS[5], PS[6], PS[3], PS[4])

                    def stX(n, kind, a_, b_):
                        sqb = T16[n % 4]
                        pn = PNB[n % 4]
                        if kind == "dn":
                            dst = (dq, dk)[a_]
                            tt("dve", sqb.ap, dst.ap[:, b_, :], dst.ap[:, b_, :], ALU.mult, [dst.b], [sqb.b])
                            mm([(pn.ap, oneb.ap, sqb.ap, True, True)], [oneb.b, sqb.b], [pn.b])
                        else:
                            j = a_
                            if j == 0:
                                sbw["w"] = ws_next((l, "in", 4))
                            wi, wsl = sbw["w"]
                            ps = PS[1 + (j % 2)]
                            proj_fm(wsl, j, ps)
                            if j == 3:
                                ws_release(wi)
                            xs = T32[(j % 4)]
                            cp("act", xs.ap, ps.ap, [ps.b], [xs.b])
                            tt("dve", sqb.ap, xs.ap, xs.ap, ALU.mult, [xs.b], [sqb.b])
                            mm([(pn.ap, blkb.ap, sqb.ap, True, True)], [blkb.b, sqb.b], [pn.b])

                    def stY(n, kind, a_, b_):
                        pn = PNB[n % 4]
                        rs = T32[4 + (n % 4)]
                        if kind == "dn":
                            dst = (dq, dk)[a_]
                            rsqrt_psum(rs, pn, 4.0 * EPS, rs)
                            if a_ == 0:
                                stt(dst.ap[:, b_, :], dst.ap[:, b_, :], 128.0 ** -0.5, rs.ap, ALU.mult, ALU.mult, [dst.b, rs.b], [dst.b])
                            else:
                                tt("dve", dst.ap[:, b_, :], dst.ap[:, b_, :], rs.ap, ALU.mult, [dst.b, rs.b], [dst.b])
                        else:
                            j = a_
                            xs = T32[(j % 4)]
                            rsqrt_psum(rs, pn, 64.0 * EPS, rs)
                            if j < 2:
                                stt(sbq.ap[:, j, :], xs.ap, drv.ap[:, 21:22], rs.ap, ALU.mult, ALU.mult, [xs.b, drv.b, rs.b], [sbq.b])
                            else:
                                stt(sbk.ap[:, j - 2, c0:c0 + 512], xs.ap, drv.ap[:, 22:23], rs.ap, ALU.mult, ALU.mult, [xs.b, drv.b, rs.b], [sbk.b])

                    for n in range(len(l2items) + 2):
                        if n < len(l2items):
                            stX(n, *l2items[n])
                        if n >= 2:
                            stY(n - 2, *l2items[n - 2])

                    kb.stage = "SGmix"
                    for t4 in range(4):
                        pm = PS[1 + (t4 % 2)]
                        for g in range(4):
                            mm([(pm.ap[:, g * 64:(g + 1) * 64], sgwm.ap[:, g, :], sgv.ap[:, t4, g * 64:(g + 1) * 64], True, True)],
                               [sgwm.b, sgv.b], [pm.b])
                        og = T16[2 + (t4 % 2)]
                        for g in range(4):
                            stt(og.ap[:, g * 64:(g + 1) * 64], pm.ap[:, g * 64:(g + 1) * 64], drv.ap[:, 24 + g:25 + g], sgu.ap[:, t4, g * 64:(g + 1) * 64],
                                ALU.add, ALU.mult, [pm.b, drv.b, sgu.b], [og.b])
                        tr([(PB.ap[:, jj * 128:(jj + 1) * 128], og.ap[:, jj * 128:(jj + 1) * 128], idb.ap) for jj in range(2)], [og.b, idb.b], [PB.b])
                        cp("act", oT.ap[:, 6:8, t4 * 128:(t4 + 1) * 128], PB.ap[:, 0:256].rearrange("p (a b) -> p a b", a=2), [PB.b], [oT.b])

                    kb.stage = "DNprep"
                    a_in = ab[:, :, 0:4]
                    b_in = ab[:, :, 4:8]
                    def smr(i):
                        return sm.ap[:, i, :].rearrange("p (t e) -> p t e", e=4)
                    g_t, beta, nbeta, vb, eg, el, bq, tmpa, tmpb = [smr(i) for i in range(4, 13)]
                    dtb_b = pl.ap[:, P_DTB:P_DTB + 4].unsqueeze(1).to_broadcast([128, 4, 4])
                    nA_b = drv.ap[:, 16:20].unsqueeze(1).to_broadcast([128, 4, 4])
                    tt("dve", tmpa, a_in, dtb_b, ALU.add, [sm.b, pl.b], [sm.b])
                    act(tmpa, tmpa, AF.Exp, [sm.b], [sm.b])
                    act(tmpa, tmpa, AF.Ln, [sm.b], [sm.b], bias=1.0)
                    tt("dve", g_t, tmpa, nA_b, ALU.mult, [sm.b, drv.b], [sm.b])
                    act(tmpb, b_in, AF.Exp, [sm.b], [sm.b], scale=-1.0)
                    ts("dve", tmpb, tmpb, 1.0, None, ALU.add, None, [sm.b], [sm.b])
                    kb.op("dve", lambda h, o=beta, i=tmpb: h.reciprocal(out=o, in_=i), reads=[sm.b], writes=[sm.b])
                    ts("dve", nbeta, beta, -1.0, None, ALU.mult, None, [sm.b], [sm.b])
                    ts("dve", vb, beta, 0.5, None, ALU.mult, None, [sm.b], [sm.b])
                    pgc = PS[0]
                    specs = []
                    for t4 in range(4):
                        specs.append((pgc.ap[:, t4 * 8:t4 * 8 + 4], cs(C_TRI), g_t[:, t4, :], True, True))
                        specs.append((pgc.ap[:, t4 * 8 + 4:t4 * 8 + 8], cs(C_BLK), g_t[:, t4, :], True, True))
                    mm(specs, [cst.b, sm.b], [pgc.b])
                    cp("dve", gg.ap, pgc.ap[:, 0:32].rearrange("p (t e) -> p t e", e=8), [pgc.b], [gg.b])
                    gc = gg.ap[:, :, 0:4]
                    gl = gg.ap[:, :, 4:8]
                    act(eg, gc, AF.Exp, [gg.b], [sm.b])
                    tt("dve", tmpa, gl, gc, ALU.subtract, [gg.b], [sm.b])
                    act(el, tmpa, AF.Exp, [sm.b], [sm.b])
                    tt("dve", bq, beta, eg, ALU.mult, [sm.b], [sm.b])

                    def dn_prep(t4):
                        kb.stage = "DNtile"
                        par = t4 % 2
                        glbT = glb if par == 0 else glb2
                        tc = slice(t4 * 128, (t4 + 1) * 128)
                        Gb, A, B, C_, D_, E_, F_, X_ = T32

                        def v4(t):
                            return t.ap.rearrange("p (h n) -> p h n", h=4)

                        def bc4(ap):
                            return ap.unsqueeze(2).to_broadcast([128, 4, 128])

                        def cb4(off):
                            return cs(off).unsqueeze(1).to_broadcast([128, 4, 128])

                        def hsl(h):
                            return slice(h * 128, (h + 1) * 128)

                        cp("act", v4(Gb), bc4(g_t[:, t4, :]), [sm.b], [Gb.b])
                        mm([(PS[4].ap[:, hsl(h)], v4(Gb)[:, h, :], cs(C_TRI), True, True) for h in range(4)], [Gb.b, cst.b], [PS[4].b])
                        mm([(PS[2].ap[:, h * 2:h * 2 + 2], v4(Gb)[:, h, :], cst.ap[:, C_BLK:C_BLK + 128:64], True, True) for h in range(4)],
                           [Gb.b, cst.b], [PS[2].b])
                        act(glbT.ap[:, 0:8], PS[2].ap[:, 0:8], AF.Exp, [PS[2].b], [glbT.b])
                        for h in range(4):
                            ts("dve", A.ap[:, hsl(h)], PS[4].ap[:, hsl(h)], gc[:, t4, h:h + 1], 0.0, ALU.subtract, ALU.max, [PS[4].b, gg.b], [A.b])
                        act(B.ap, A.ap, AF.Exp, [A.b], [B.b], scale=-1.0)
                        tt("dve", v4(D_), v4(B), cb4(C_MS), ALU.mult, [B.b, cst.b], [D_.b])
                        tt("pool", v4(C_), v4(B), cb4(C_MI), ALU.mult, [B.b, cst.b], [C_.b])
                        act(A.ap, PS[4].ap, AF.Exp, [PS[4].b], [A.b])
                        qd = T16[0] if par == 0 else T16b[0]
                        tt("pool", v4(qd), dq.ap[:, :, tc], v4(A), ALU.mult, [dq.b, A.b], [qd.b])
                        mm([(PS[2].ap[:, hsl(h)], dk.ap[:, h, tc], dk.ap[:, h, tc], True, True) for h in range(4)], [dk.b], [PS[2].b])
                        mm([(PS[3].ap[:, hsl(h)], dq.ap[:, h, tc], dk.ap[:, h, tc], True, True) for h in range(4)], [dq.b, dk.b], [PS[3].b])
                        for h in range(4):
                            stt(E_.ap[:, hsl(h)], PS[2].ap[:, hsl(h)], nbeta[:, t4, h:h + 1], D_.ap[:, hsl(h)], ALU.mult, ALU.mult,
                                [PS[2].b, sm.b, D_.b], [E_.b])
                        qkm = T16[1]
                        tt("dve", qkm.ap, PS[3].ap, C_.ap, ALU.mult, [PS[3].b, C_.b], [qkm.b])
                        tr([(PS[4].ap[:, hsl(h)], v4(E_)[:, h, :], ident) for h in range(4)], [E_.b, cst.b], [PS[4].b])
                        cp("act", F_.ap, PS[4].ap, [PS[4].b], [F_.b])
                        Xs = X_.ap[:, 0:256].rearrange("p (h n) -> p h n", h=4)
                        for hb in range(2):
                            hs_ = slice(hb * 64, hb * 64 + 64)
                            tt("dve", Xs[hs_], PS[4].ap.rearrange("p (h n) -> p h n", h=4)[hs_, :, hb * 64:hb * 64 + 64],
                               ident[hs_, hb * 64:hb * 64 + 64].unsqueeze(1).to_broadcast([64, 4, 64]), ALU.add, [PS[4].b, cst.b], [X_.b])
                        Qc, Pc = E_, F_
                        Qn, Pn = C_, D_
                        for lev in range(1, 6):
                            mm([(PS[2].ap[:, hsl(h)], v4(Pc)[:, h, :], v4(Qc)[:, h, :], True, True) for h in range(4)], [Pc.b, Qc.b], [PS[2].b])
                            if lev < 5:
                                mm([(PS[3].ap[:, hsl(h)], v4(Qc)[:, h, :], v4(Pc)[:, h, :], True, True) for h in range(4)], [Pc.b, Qc.b], [PS[3].b])
                            cp("act", Qn.ap, PS[2].ap, [PS[2].b], [Qn.b])
                            if lev < 5:
                                cp("dve", Pn.ap, PS[3].ap, [PS[3].b], [Pn.b])
                            mm([(PS[4].ap[:, h * 64:(h + 1) * 64], v4(Qn)[:, h, :], Xs[:, h, :], True, True) for h in range(4)], [Qn.b, X_.b], [PS[4].b])
                            tt("dve", X_.ap[:, 0:256], X_.ap[:, 0:256], PS[4].ap[:, 0:256], ALU.add, [X_.b, PS[4].b], [X_.b])
                            Qc, Qn = Qn, Qc
                            Pc, Pn = Pn, Pc
                        Xbd = Pn
                        for hb in range(2):
                            act(v4(Xbd)[:, :, hb * 64:hb * 64 + 64], Xs, AF.Copy, [X_.b, cst.b], [Xbd.b], scale=cst.ap[:, C_BLK + hb * 64:C_BLK + hb * 64 + 1])
                        X_ = Xbd
                        tr([(PB.ap[:, hsl(h)], dk.ap[:, h, tc], idb.ap) for h in range(4)], [dk.b, idb.b], [PB.b])
                        Kbe = A
                        tt("dve", v4(Kbe), PB.ap[:, 0:512].rearrange("p (h n) -> p h n", h=4), bc4(bq[:, t4, :]), ALU.mult, [PB.b, sm.b], [Kbe.b])
                        kdec = T16[3] if par == 0 else T16b[1]
                        tt("dve", v4(kdec), PB.ap[:, 0:512].rearrange("p (h n) -> p h n", h=4), bc4(el[:, t4, :]), ALU.mult, [PB.b, sm.b], [kdec.b])
                        tr([(PB.ap[:, 512 + h * 128:512 + (h + 1) * 128], dv.ap[:, h, tc], idb.ap) for h in range(4)], [dv.b, idb.b], [PB.b])
                        Vb = B
                        tt("dve", v4(Vb), PB.ap[:, 512:1024].rearrange("p (h n) -> p h n", h=4), bc4(vb[:, t4, :]), ALU.mult, [PB.b, sm.b], [Vb.b])
                        tr([(PB.ap[:, hsl(h)], v4(qkm)[:, h, :], idb.ap) for h in range(4)], [qkm.b, idb.b], [PB.b])
                        qkT = T16[4] if par == 0 else T16b[2]
                        cp("act", qkT.ap, PB.ap[:, 0:512], [PB.b], [qkT.b])
                        mm([(PS[2].ap[:, hsl(h)], v4(Kbe)[:, h, :], v4(X_)[:, h, :], True, True) for h in range(4)], [Kbe.b, X_.b], [PS[2].b])
                        WT = T16[2] if par == 0 else T16b[3]
                        cp("act", WT.ap, PS[2].ap, [PS[2].b], [WT.b])
                        mm([(PS[3].ap[:, hsl(h)], v4(X_)[:, h, :], v4(Vb)[:, h, :], True, True) for h in range(4)], [X_.b, Vb.b], [PS[3].b])
                        U = Upar[par]
                        cp("act", U.ap, PS[3].ap, [PS[3].b], [U.b])
                        return dict(qd=qd, kdec=kdec, qkT=qkT, WT=WT, U=U, glbT=glbT)

                    def dn_scan(t4, tb):
                        qd, kdec, qkT, WT, U, glbT = tb['qd'], tb['kdec'], tb['qkT'], tb['WT'], tb['U'], tb['glbT']

                        def v4(t):
                            return t.ap.rearrange("p (h n) -> p h n", h=4)

                        def hsl(h):
                            return slice(h * 128, (h + 1) * 128)

                        for half in range(2):
                            kb.stage = "DNscan"
                            hs = slice(half * 64, half * 64 + 64)
                            tcs = slice(t4 * 128 + half * 64, t4 * 128 + half * 64 + 64)
                            Sbf, SbfN = Sbfs[half], Sbfs[1 - half]
                            mm([(PS[5].ap[:, hsl(h)], v4(WT)[:, h, :], Sbf.ap[:, h, :], True, True) for h in range(4)], [WT.b, Sbf.b], [PS[5].b])
                            un = T16[5]
                            tt("dve", un.ap[hs, :], U.ap[hs, :], PS[5].ap[hs, :], ALU.subtract, [U.b, PS[5].b], [un.b])
                            mm([(PS[0].ap[:, hsl(h)], v4(kdec)[hs, h, :], v4(un)[hs, h, :], True, True) for h in range(4)], [kdec.b, un.b], [PS[0].b])
                            specs = []
                            for h in range(4):
                                specs.append((PS[6].ap[:, h * 64:(h + 1) * 64], Sbf.ap[:, h, :], v4(qd)[:, h, hs], True, False))
                                specs.append((PS[6].ap[:, h * 64:(h + 1) * 64], v4(un)[hs, h, :], v4(qkT)[hs, h, hs], False, True))
                            mm(specs, [Sbf.b, qd.b, un.b, qkT.b], [PS[6].b])
                            cp("act", oraw.ap[:, :, tcs], PS[6].ap[:, 0:256].rearrange("p (h n) -> p h n", h=4), [PS[6].b], [oraw.b])
                            Sd = SdT
                            tt("pool", v4(Sd), S32.ap, glbT.ap[:, 0:8].rearrange("p (h f) -> p h f", f=2)[:, :, half:half + 1].to_broadcast([128, 4, 128]),
                               ALU.mult, [S32.b, glbT.b], [Sd.b])
                            tt("dve", SbfN.ap, v4(Sd), PS[0].ap.rearrange("p (h n) -> p h n", h=4), ALU.add, [Sd.b, PS[0].b], [SbfN.b])
                            tt("dve", S32.ap, v4(Sd), PS[0].ap.rearrange("p (h n) -> p h n", h=4), ALU.add, [Sd.b, PS[0].b], [S32.b])

                    preps, scans, tbs = [], [], []
                    for t4 in range(4):
                        kb.cap_begin()
                        tbs.append(dn_prep(t4))
                        preps.append(kb.cap_end())
                    for t4 in range(4):
                        kb.cap_begin()
                        dn_scan(t4, tbs[t4])
                        scans.append(kb.cap_end())
                    for rec in preps[0]:
                        kb.emit(rec)
                    for t4 in range(4):
                        A_, B_ = scans[t4], (preps[t4 + 1] if t4 < 3 else [])
                        ia = ib = 0
                        while ia < len(A_) or ib < len(B_):
                            if ib < len(B_) and (ia >= len(A_) or ib * len(A_) <= ia * len(B_)):
                                kb.emit(B_[ib])
                                ib += 1
                            else:
                                kb.emit(A_[ia])
                                ia += 1
                    jstate["bank"] = None
                    kb.stage = "DNout"
                    for h in range(4):
                        sqb = T16[h % 2]
                        act(sqb.ap, oraw.ap[:, h, :], AF.Square, [oraw.b], [sqb.b])
                        pn = PS[1 + (h % 2)]
                        mm([(pn.ap, oneb.ap, sqb.ap, True, True)], [oneb.b, sqb.b], [pn.b])
                        rs = T32[h % 2]
                        rsqrt_psum(rs, pn, EPS, rs, scale=1.0 / 128.0)
                        o2 = T32[2 + (h % 2)]
                        tt("pool", o2.ap, oraw.ap[:, h, :], rs.ap, ALU.mult, [oraw.b, rs.b], [o2.b])
                        stt(oT.ap[:, h, :], o2.ap, drv.ap[:, 20:21], dz.ap[:, h, :], ALU.mult, ALU.mult, [o2.b, drv.b, dz.b], [oT.b])

                    kb.stage = "ATTN"
                    wb2 = T(T32[7].ap.bitcast(BF16)[:, 0:512], T32[7].b)
                    acc = T32[0]
                    nb = 4 * c + 4
                    pairs = [(h, bi, b) for h in range(4) for bi, b in enumerate(range(nb - 1, -1, -1))]
                    NP_ = len(pairs)

                    def geo(n):
                        h, bi, b = pairs[n]
                        r = b - 4 * c
                        q0 = max(r, 0) * 128
                        qs = slice(q0, 512)
                        nm = nmask.ap[:, 384:896 - 128 * r] if r >= 0 else None
                        hp = slice((h % 2) * 64, (h % 2) * 64 + 64)
                        hc = h // 2
                        kT_ = sbk.ap[hp, hc, b * 128:(b + 1) * 128]
                        q_ = sbq.ap[hp, hc, qs]
                        return h, bi, b, r, q0, qs, nm, hp, hc, kT_, q_

                    def k0(n):
                        h, bi, b, r, q0, qs, nm, hp, hc, kT_, q_ = geo(n)
                        pz = PS[1 + (n % 2)]
                        specs = [(pz.ap[:, qs], kT_, q_, True, nm is None)]
                        rd = [sbk.b, sbq.b]
                        if nm is not None:
                            specs.append((pz.ap[:, qs], idb.ap, nm, False, True))
                            rd += [idb.b, nmask.b]
                        mm(specs, rd, [pz.b])
                        e_, sp_ = T32[1 + (n % 2)], T32[3 + (n % 4)]
                        act(e_.ap[:, qs], pz.ap[:, qs], AF.Exp, [pz.b], [e_.b])
                        act(sp_.ap[:, qs], e_.ap[:, qs], AF.Ln, [e_.b], [sp_.b], bias=1.0)

                    def k1(n):
                        h, bi, b, r, q0, qs, nm, hp, hc, kT_, q_ = geo(n)
                        sp_ = T32[3 + (n % 4)]
                        spb = T16[n % 3]
                        cp("dve", spb.ap[:, qs], sp_.ap[:, qs], [sp_.b], [spb.b])
                        if b > 0:
                            if bi == 0:
                                if q0 > 0:
                                    kb.op("pool", lambda h_, a=acc.ap: h_.memset(a, 0.0), writes=[acc.b])
                                cp("pool", acc.ap[:, qs], sp_.ap[:, qs], [sp_.b], [acc.b])
                            else:
                                tt("pool", acc.ap[:, qs], acc.ap[:, qs], sp_.ap[:, qs], ALU.add, [acc.b, sp_.b], [acc.b])

                    def k2(n):
                        h, bi, b, r, q0, qs, nm, hp, hc, kT_, q_ = geo(n)
                        spb = T16[n % 3]
                        accb = T16[3 + (n % 2)]
                        accn = T16[3 + ((n + 1) % 2)]
                        pr_ = PS[3 + (n % 2)]
                        specs = [(pr_.ap[:, qs], kT_, q_, True, False),
                                 (pr_.ap[:, qs], ntrib.ap, spb.ap[:, qs], False, bi == 0 and nm is None)]
                        rd = [sbk.b, sbq.b, ntrib.b, spb.b]
                        if bi > 0:
                            specs.append((pr_.ap[:, qs], noneb.ap, accb.ap[:, qs], False, nm is None))
                            rd += [accb.b, noneb.b]
                        if nm is not None:
                            specs.append((pr_.ap[:, qs], idb.ap, nm, False, True))
                            rd += [idb.b, nmask.b]
                        mm(specs, rd, [pr_.b])
                        if b > 0:
                            cp("dve", accn.ap, acc.ap, [acc.b], [accn.b])

                    def k3(n):
                        h, bi, b, r, q0, qs, nm, hp, hc, kT_, q_ = geo(n)
                        e_, sp_ = T32[1 + (n % 2)], T32[3 + (n % 4)]
                        pr_ = PS[3 + (n % 2)]
                        tt("dve", e_.ap[:, qs], pr_.ap[:, qs], sp_.ap[:, qs], ALU.subtract, [pr_.b, sp_.b], [e_.b])
                        wb = T16[5] if n % 2 == 0 else wb2
                        act(wb.ap[:, qs], e_.ap[:, qs], AF.Exp, [e_.b], [wb.b])

                    def k4(n):
                        h, bi, b, r, q0, qs, nm, hp, hc, kT_, q_ = geo(n)
                        wb = T16[5] if n % 2 == 0 else wb2
                        po = PS[0] if h % 2 == 0 else PS[6]
                        mm([(po.ap[:, qs], sbv.ap[:, b, hc * 128:(hc + 1) * 128], wb.ap[:, qs], bi == 0, b == 0)], [sbv.b, wb.b], [po.b], sgc=True)
                        if b == 0:
                            cp("act", oT.ap[hp, 4 + hc, :], po.ap[hp, :], [po.b], [oT.b])

                    stages_ = (k0, k1, k2, k3, k4)
                    for tck in range(NP_ + 4):
                        for kk_ in (4, 3, 2, 1, 0):
                            n = tck - kk_
                            if 0 <= n < NP_:
                                jstate["bank"], jstate["n"] = (PS[5], 1) if kk_ in (0, 2) else (None, 0)
                                stages_[kk_](n)
                    jstate["bank"] = None

                    kb.stage = "A7wout"
                    if c == 0:
                        for cg in range(2):
                            wout_i[cg], wout_t[cg] = ws_next((l, "out", cg))
                    for oc in range(8):
                        ps = PS[1 + (oc % 3)]
                        wt = wout_t[oc // 4]
                        mm([(ps.ap, wt.ap[:, k, (oc % 4) * 128:(oc % 4 + 1) * 128], oT.ap[:, k, :], k == 0, k == 7) for k in range(8)], [wt.b, oT.b], [ps.b])
                        tt("dve", xT.ap[:, oc, c0:c0 + 512], xT.ap[:, oc, c0:c0 + 512], ps.ap, ALU.add, [xT.b, ps.b], [xT.b])
                for cg in range(2):
                    ws_release(wout_i[cg])

                conv_upto(l, "f2")
                kb.stage = "Fnorm"
                kb.barrier()
                ar = Arena(SCR0, MEMB)
                pl = ar.a("plF", NPL, F32)
                drv = ar.a("drvF", 64, F32)
                h2 = ar.a("h2", 8 * S, BF16, (8, S))
                aT = ar.a("aT", 8 * 512, BF16, (8, 512))
                aT2 = ar.a("aT2", 8 * 512, BF16, (8, 512))
                T32 = [ar.a("t32F_%d" % i, 512, F32) for i in range(4)]
                kb.dma("sp", pl.ap, pl_d[l], pl.b, writes=[pl.b])
                ts("dve", drv.ap[:, 0:16], pl.ap[:, P_G1:P_G1 + 16], 32.0, None, ALU.mult, None, [pl.b], [drv.b])
                for c in range(NCH):
                    c0 = c * 512
                    act(aT.ap, xT.ap[:, :, c0:c0 + 512], AF.Square, [xT.b], [aT.b])
                    mm([(PS[0].ap, oneb.ap, aT.ap[:, k, :], k == 0, k == 7) for k in range(8)], [oneb.b, aT.b], [PS[0].b])
                    rsqrt_psum(T32[0], PS[0], 1024.0 * EPS, T32[1])
                    for k in range(8):
                        stt(h2.ap[:, k, c0:c0 + 512], xT.ap[:, k, c0:c0 + 512], drv.ap[:, 8 + k:9 + k], T32[0].ap,
                            ALU.mult, ALU.mult, [xT.b, drv.b, T32[0].b], [h2.b])
                kb.stage = "FFN"
                aTs = [aT, aT2]
                work = [(fg, c) for fg in range(4) for c in range(NCH)]
                held = {}

                def ff1(n, fg, c):
                    if c == 0:
                        conv_next(3)
                        if fg == 3 and l + 1 < L:
                            conv_upto(l + 1, "out")
                        held[fg] = [ws_next((l, "f1", 2 * fg)), ws_next((l, "f1", 2 * fg + 1))]
                    w1a, w1b = held[fg][0][1], held[fg][1][1]
                    c0 = c * 512
                    at = aTs[n % 2]
                    for f in range(8):
                        w1 = w1a if f < 4 else w1b
                        ps = PS[f % 3]
                        mm([(ps.ap, w1.ap[:, k, (f % 4) * 128:(f % 4 + 1) * 128], h2.ap[:, k, c0:c0 + 512], k == 0, k == 7) for k in range(8)],
                           [w1.b, h2.b], [ps.b])
                        rl = T32[2 + (f % 2)]
                        act(rl.ap, ps.ap, AF.Relu, [ps.b], [rl.b])
                        act(at.ap[:, f, :], rl.ap, AF.Square, [rl.b], [at.b])
                    if c == NCH - 1:
                        ws_release(held[fg][0][0])
                        ws_release(held[fg][1][0])

                def ff2(n, fg, c):
                    if c == 0:
                        held[fg] += [ws_next((l, "f2", fg, 0)), ws_next((l, "f2", fg, 1))]
                    w2a, w2b = held[fg][2][1], held[fg][3][1]
                    c0 = c * 512
                    at = aTs[n % 2]
                    for oc in range(8):
                        w2 = w2a if oc < 4 else w2b
                        ps = PS[3 + (oc % 3)]
                        mm([(ps.ap, w2.ap[:, k, (oc % 4) * 128:(oc % 4 + 1) * 128], at.ap[:, k, :], k == 0, k == 7) for k in range(8)],
                           [w2.b, at.b], [ps.b])
                        tt("dve", xT.ap[:, oc, c0:c0 + 512], xT.ap[:, oc, c0:c0 + 512], ps.ap, ALU.add, [xT.b, ps.b], [xT.b])
                    if c == NCH - 1:
                        for (i_, _t) in held.pop(fg)[2:]:
                            ws_release(i_)

                prevw = None
                for n, wk in enumerate(work + [None]):
                    if wk is not None:
                        ff1(n, *wk)
                    if prevw is not None:
                        ff2(n - 1, *prevw)
                    prevw = wk

            kb.stage = "store"
            kb.barrier()
            ar = Arena(SCR0, MEMB)
            yo = [ar.a("yo%d" % i, 1024, F32) for i in range(2)]
            for t in range(NT):
                yt = yo[t % 2]
                for hf in range(2):
                    ps = PS[(2 * t + hf) % 4]
                    tr([(ps.ap[:, j * 128:(j + 1) * 128], xT.ap[:, hf * 4 + j, t * 128:(t + 1) * 128], ident) for j in range(4)],
                       [xT.b, cst.b], [ps.b])
                    cp("act" if hf else "dve", yt.ap[:, hf * 512:(hf + 1) * 512], ps.ap, [ps.b], [yt.b])
                kb.dma("sp", y_d[seq, t * 128:(t + 1) * 128, :], yt.ap, yt.b, reads=[yt.b])

        with nc.Block() as block:
            kb.finish(block)
        print("prog sizes:", {n: len(e.prog) for n, e in kb.eng.items()}, "sems:", len(kb.sems))
        global LAST_LABELS
        LAST_LABELS = {n: list(e.labels) for n, e in kb.eng.items()}
    return nc


LAST_LABELS = None


_NC_CACHE = {}


def run(inputs, NSEQ, S, L, n_cores):
    key = (NSEQ, S, L)
    if key not in _NC_CACHE:
        _NC_CACHE[key] = build(NSEQ, S, L)
    nc = _NC_CACHE[key]
    f32 = lambda a: np.ascontiguousarray(np.asarray(a, dtype=np.float32))
    inp = {k: f32(v) for k, v in inputs.items()}
    pl = make_params(inp, L)
    cst = make_consts()
    x = inp["x"]
    in_maps = []
    for ci in range(n_cores):
        in_maps.append({
            "x": np.ascontiguousarray(x[ci * NSEQ:(ci + 1) * NSEQ]),
            "w_in": inp["w_in"], "w_out": inp["w_out"], "w_ff1": inp["w_ff1"], "w_ff2": inp["w_ff2"],
            "pl": pl, "cst": cst,
        })
    res = run_bass_kernel_spmd(nc, in_maps, core_ids=list(range(n_cores)))
    return np.concatenate([np.asarray(r["y"]) for r in res.results], axis=0)


def kernel(**inputs):
    x = np.asarray(inputs["x"])
    B, S, D = x.shape
    L = np.asarray(inputs["w_in"]).shape[0]
    NSEQ = B // N_CORES
    return run(inputs, NSEQ, S, L, N_CORES).astype(np.float32)
```
